# Optimizing a Trainium2 kernel written in Bass

```python
import math
import jax, jax.numpy as jnp
from jax import lax
import numpy as np

D_MODEL = 1024
BATCH = 4
SEQ = 8192
DEPTH = 1

HEAD_DIM = 128
HEADS_PER_GROUP = 4
ATTN_PATTERNS = ((128, 1), (512, 4), (2048, 16))
N_ATTN_HEADS = HEADS_PER_GROUP * len(ATTN_PATTERNS)
ATTN_QKV_WIDTH = N_ATTN_HEADS * HEAD_DIM
ATTN_OUT_WIDTH = HEADS_PER_GROUP * HEAD_DIM
FOURIER_GROUPS = 4
FOURIER_GROUP_DIM = 128
FOURIER_WIDTH = FOURIER_GROUPS * FOURIER_GROUP_DIM
N_BRANCHES = 2
IN_WIDTH = 3 * ATTN_QKV_WIDTH + FOURIER_WIDTH + N_BRANCHES * D_MODEL
D_FF = 2752
NUM_BUCKETS = 32
MAX_EXACT = 8
MAX_DISTANCE = 1024
NEG_INF = -1e30
LN_EPS = 1e-5
ALPHA = (2 * DEPTH) ** 0.25
BETA = (8 * DEPTH) ** -0.25

kernel_name = "hybrid_dilated_attn_fnet_macaron_encoder"


def layer_norm(x, g, b):
    xf = x.astype(jnp.float32)
    mu = jnp.mean(xf, axis=-1, keepdims=True)
    var = jnp.mean(jnp.square(xf - mu), axis=-1, keepdims=True)
    y = (xf - mu) * lax.rsqrt(var + LN_EPS) * g.astype(jnp.float32) + b.astype(jnp.float32)
    return y.astype(x.dtype)


def swiglu(x, w_gate, w_up, w_down):
    return (jax.nn.silu(x @ w_gate) * (x @ w_up)) @ w_down


def t5_bucket(rel):
    half = NUM_BUCKETS // 2
    ret = (rel > 0).astype(jnp.int32) * half
    n = jnp.abs(rel)
    nf = jnp.maximum(n, 1).astype(jnp.float32)
    large = MAX_EXACT + (jnp.log(nf / MAX_EXACT) / math.log(MAX_DISTANCE / MAX_EXACT)
                         * (half - MAX_EXACT)).astype(jnp.int32)
    large = jnp.minimum(large, half - 1)
    return ret + jnp.where(n < MAX_EXACT, n, large)


def dilated_window_attention(q, k, v, rel_bias_g, window, dilation):
    B, S, H, E = q.shape
    half = window // (2 * dilation)
    blk = half
    L = S // dilation
    nb = -(-L // blk)
    Lp = nb * blk
    scale = E ** -0.5

    def to_sub(t):
        return t.reshape(B, L, dilation, H, E).transpose(0, 2, 3, 1, 4)

    qs, ks, vs = to_sub(q), to_sub(k), to_sub(v)
    qb = jnp.pad(qs, ((0, 0),) * 3 + ((0, Lp - L), (0, 0))).reshape(B, dilation, H, nb, blk, E)

    def windows(t):
        tp = jnp.pad(t, ((0, 0),) * 3 + ((blk, Lp - L + blk), (0, 0)))
        tp = tp.reshape(B, dilation, H, nb + 2, blk, E)
        return jnp.concatenate([tp[:, :, :, :-2], tp[:, :, :, 1:-1], tp[:, :, :, 2:]], axis=4)

    kw, vw = windows(ks), windows(vs)

    a_idx = jnp.arange(blk, dtype=jnp.int32)[:, None]
    b_idx = jnp.arange(3 * blk, dtype=jnp.int32)[None, :]
    off = b_idx - blk - a_idx
    band = jnp.abs(off) <= half
    key_pos = jnp.arange(nb, dtype=jnp.int32)[:, None] * blk - blk + jnp.arange(3 * blk, dtype=jnp.int32)[None, :]
    valid = (key_pos >= 0) & (key_pos < L)
    mask = band[None] & valid[:, None, :]
    bias = rel_bias_g[t5_bucket(off * dilation)]
    bias = jnp.moveaxis(bias, -1, 0).astype(jnp.float32)
    logit_add = jnp.where(mask[None], bias[:, None], NEG_INF)

    s = jnp.einsum('bdhnqe,bdhnke->bdhnqk', qb, kw).astype(jnp.float32) * scale + logit_add
    mx = jnp.max(s, axis=-1, keepdims=True)
    p = jnp.exp(s - mx)
    den = jnp.sum(p, axis=-1)
    o = jnp.einsum('bdhnqk,bdhnke->bdhnqe', p, vw.astype(jnp.float32)) / den[..., None]
    lse = mx[..., 0] + jnp.log(den)

    o = o.reshape(B, dilation, H, Lp, E)[:, :, :, :L].transpose(0, 3, 1, 2, 4).reshape(B, S, H, E)
    lse = lse.reshape(B, dilation, H, Lp)[:, :, :, :L].transpose(0, 3, 1, 2).reshape(B, S, H)
    return o, lse


def hybrid_mixer(x, w_in, b_in, rel_bias, w_proj_attn, w_proj_fourier, w_out):
    B, S, _ = x.shape
    h = x @ w_in + b_in
    A = ATTN_QKV_WIDTH
    q = h[..., 0:A].reshape(B, S, N_ATTN_HEADS, HEAD_DIM)
    k = h[..., A:2 * A].reshape(B, S, N_ATTN_HEADS, HEAD_DIM)
    v = h[..., 2 * A:3 * A].reshape(B, S, N_ATTN_HEADS, HEAD_DIM)
    u = h[..., 3 * A:3 * A + FOURIER_WIDTH]
    gates = jax.nn.sigmoid(h[..., 3 * A + FOURIER_WIDTH:])

    outs, lses = [], []
    for g, (window, dilation) in enumerate(ATTN_PATTERNS):
        sl = slice(g * HEADS_PER_GROUP, (g + 1) * HEADS_PER_GROUP)
        o, l = dilated_window_attention(q[:, :, sl], k[:, :, sl], v[:, :, sl],
                                        rel_bias[:, sl], window, dilation)
        outs.append(o)
        lses.append(l)
    w_mix = jax.nn.softmax(jnp.stack(lses, axis=0), axis=0)
    attn = jnp.sum(w_mix[..., None] * jnp.stack(outs, axis=0), axis=0)
    attn = attn.reshape(B, S, ATTN_OUT_WIDTH).astype(x.dtype)

    uf = u.astype(jnp.float32).reshape(B, S, FOURIER_GROUPS, FOURIER_GROUP_DIM)
    four = jnp.real(jnp.fft.fft2(uf, axes=(1, 3), norm='ortho'))
    four = four.reshape(B, S, FOURIER_WIDTH).astype(x.dtype)

    g_attn = gates[..., :D_MODEL]
    g_four = gates[..., D_MODEL:]
    merged = g_attn * (attn @ w_proj_attn) + g_four * (four @ w_proj_fourier)
    return merged @ w_out


def setup_inputs(seed: int = 0) -> dict:
    key = jax.random.key(seed)
    ks = jax.random.split(key, 20)

    def nrm(k, shape, scale):
        return jax.random.normal(k, shape, jnp.float32) * scale

    D, F = D_MODEL, D_FF
    return {
        "x": nrm(ks[0], (BATCH, SEQ, D), 1.0),
        "ln1_g": 1.0 + nrm(ks[1], (DEPTH, D), 0.02),
        "ln1_b": nrm(ks[2], (DEPTH, D), 0.02),
        "ffn1_w_gate": nrm(ks[3], (DEPTH, D, F), D ** -0.5),
        "ffn1_w_up": nrm(ks[4], (DEPTH, D, F), D ** -0.5),
        "ffn1_w_down": nrm(ks[5], (DEPTH, F, D), F ** -0.5 * BETA),
        "w_in": nrm(ks[6], (DEPTH, D, IN_WIDTH), D ** -0.5),
        "b_in": nrm(ks[7], (DEPTH, IN_WIDTH), 0.02),
        "rel_bias": nrm(ks[8], (NUM_BUCKETS, N_ATTN_HEADS), 0.5),
        "w_proj_attn": nrm(ks[9], (DEPTH, ATTN_OUT_WIDTH, D), ATTN_OUT_WIDTH ** -0.5),
        "w_proj_fourier": nrm(ks[10], (DEPTH, FOURIER_WIDTH, D), FOURIER_WIDTH ** -0.5),
        "w_out": nrm(ks[11], (DEPTH, D, D), D ** -0.5 * BETA),
        "ln2_g": 1.0 + nrm(ks[12], (DEPTH, D), 0.02),
        "ln2_b": nrm(ks[13], (DEPTH, D), 0.02),
        "ffn2_w_gate": nrm(ks[14], (DEPTH, D, F), D ** -0.5),
        "ffn2_w_up": nrm(ks[15], (DEPTH, D, F), D ** -0.5),
        "ffn2_w_down": nrm(ks[16], (DEPTH, F, D), F ** -0.5 * BETA),
        "ln3_g": 1.0 + nrm(ks[17], (DEPTH, D), 0.02),
        "ln3_b": nrm(ks[18], (DEPTH, D), 0.02),
    }


def reference(x, ln1_g, ln1_b, ffn1_w_gate, ffn1_w_up, ffn1_w_down, w_in, b_in, rel_bias,
              w_proj_attn, w_proj_fourier, w_out, ln2_g, ln2_b, ffn2_w_gate, ffn2_w_up,
              ffn2_w_down, ln3_g, ln3_b):
    h = x
    for l in range(DEPTH):
        h = layer_norm(ALPHA * h + 0.5 * swiglu(h, ffn1_w_gate[l], ffn1_w_up[l], ffn1_w_down[l]),
                       ln1_g[l], ln1_b[l])
        h = layer_norm(ALPHA * h + hybrid_mixer(h, w_in[l], b_in[l], rel_bias, w_proj_attn[l],
                                                w_proj_fourier[l], w_out[l]),
                       ln2_g[l], ln2_b[l])
        h = layer_norm(ALPHA * h + 0.5 * swiglu(h, ffn2_w_gate[l], ffn2_w_up[l], ffn2_w_down[l]),
                       ln3_g[l], ln3_b[l])
    return h
```

```python
import contextlib
import math
import numpy as np
import ml_dtypes
import concourse.bass as bass
import concourse.mybir as mybir
from concourse.bass_utils import run_bass_kernel_spmd

F32 = mybir.dt.float32
BF16 = mybir.dt.bfloat16
AF = mybir.ActivationFunctionType
ALU = mybir.AluOpType

D = 1024
DFF = 2752
S = 8192
NOWN = 4096
NKV = 5120
CH = 512
PAD = 1024
ALPHA = 2.0 ** 0.25
EPS = 1e-5
DILS = (1, 4, 16)
NFC = 22


def ss(start, n, step):
    return slice(start, start + (n - 1) * step + 1, step)


class _Any:
    def __getitem__(self, k):
        return self

    def __getattr__(self, k):
        return lambda *a, **kw: self


class Tl:
    __slots__ = ("ap", "lw", "rd", "dkey")

    def __init__(self, ap):
        self.ap = ap
        self.lw = {}
        self.rd = {}
        self.dkey = None


class Kx:
    def __init__(self, nc, es):
        self.nc = nc
        self.dry = False
        self.eng = {"pe": nc.tensor, "act": nc.scalar, "dve": nc.vector, "pool": nc.gpsimd, "sp": nc.sync}
        self.sem = {}
        self.cnt = {}
        for n in ["pe", "act", "dve", "pool", "d:w", "d:x", "d:st", "d:l2", "d:c", "d:cc"]:
            self.sem[n] = es.enter_context(nc.semaphore("s_" + n.replace(":", "_")))
            self.cnt[n] = 0
        self.seen = {e: {} for e in self.eng}
        self.es = es
        self.ntile = 0
        self.free_keys = []
        self.live_keys = []

    def tile_stream(self, t, q="sp"):
        if t.dkey is None:
            t.dkey = {}
        if q not in t.dkey:
            if self.free_keys:
                k = self.free_keys.pop()
            else:
                self.ntile += 1
                k = "t:%d" % self.ntile
                self.sem[k] = self.es.enter_context(self.nc.semaphore("s_t%d" % self.ntile))
                self.cnt[k] = 0
            self.live_keys.append(k)
            t.dkey[q] = k
        return t.dkey[q]

    def _wait(self, eng, src, n):
        if eng == "pe" and src == "pe":
            return
        if self.seen[eng].get(src, 0) >= n:
            return
        self.seen[eng][src] = n
        self.eng[eng].wait_ge(self.sem[src], n * (16 if src[1] == ":" else 1))

    def _deps(self, eng, reads, writes, disjoint):
        deps = {}
        for t in reads:
            for s, n in t.lw.items():
                deps[s] = max(deps.get(s, 0), n)
        if not disjoint:
            for t in writes:
                for s, n in t.lw.items():
                    deps[s] = max(deps.get(s, 0), n)
                for s, n in t.rd.items():
                    deps[s] = max(deps.get(s, 0), n)
        for s, n in deps.items():
            self._wait(eng, s, n)

    def _stamp(self, src, n, reads, writes, disjoint):
        for t in reads:
            t.rd[src] = max(t.rd.get(src, 0), n)
        for t in writes:
            if disjoint:
                t.lw[src] = max(t.lw.get(src, 0), n)
            else:
                t.lw = {src: n}
                t.rd = {}

    def op(self, eng, fn, reads=(), writes=(), signal=True, disjoint=False):
        if self.dry:
            return
        self._deps(eng, reads, writes, disjoint)
        ins = fn(self.eng[eng])
        if signal:
            self.cnt[eng] += 1
            ins.then_inc(self.sem[eng], 1)
            n = self.cnt[eng]
        else:
            n = self.cnt[eng] + 1
        self._stamp(eng, n, reads, writes, disjoint)

    def dma(self, q, stream, out, in_, reads=(), writes=(), disjoint=False, st=None):
        if self.dry:
            return
        if st is None:
            if len(writes) == 1 and not disjoint and len(reads) <= 1 and stream != "d:cc":
                st = writes[0]
            elif disjoint and len(reads) == 1 and stream == "d:st":
                st = reads[0]
            elif stream == "d:l2" and len(writes) == 1:
                st = writes[0]
        if st is not None:
            stream = self.tile_stream(st, q)
        self._deps(q, reads, writes, disjoint)
        ins = self.eng[q].dma_start(out=out, in_=in_)
        self.cnt[stream] += 1
        ins.then_inc(self.sem[stream], 16)
        self._stamp(stream, self.cnt[stream], reads, writes, disjoint)

    def barrier(self):
        if self.dry:
            return
        for e in self.eng:
            for src in self.sem:
                if self.cnt[src] > 0:
                    self._wait(e, src, self.cnt[src])
        self.free_keys.extend(self.live_keys)
        self.live_keys = []

    def wait_all(self, eng, tiles):
        if self.dry:
            return
        self._deps(eng, tiles, (), False)


class WRing:
    def __init__(self, kx, slots):
        self.kx = kx
        self.slots = slots
        self.plan = []
        self.idx = 0
        self.issued = 0

    def reset(self):
        self.idx = 0
        self.issued = 0

    def next(self, parts):
        if self.kx.dry:
            self.plan.append(parts)
            return Tl(_Any())
        R = len(self.slots)
        while self.issued < min(self.idx + R - 1, len(self.plan)):
            j = self.issued
            sl = self.slots[j % R]
            for pi_, (dst, src, srcT) in enumerate(self.plan[j]):
                self.kx.dma("sp", "d:w", dst(sl.ap), src, reads=[srcT], writes=[sl], disjoint=(pi_ > 0), st=sl)
            self.issued += 1
        sl = self.slots[self.idx % R]
        self.idx += 1
        return sl


def build_nc():
    import os
    KPIPE = int(os.environ.get('KPIPE', '1'))
    KDISJ = int(os.environ.get('KDISJ', '1'))
    KSER = int(os.environ.get('KSER', '1'))
    KA2 = int(os.environ.get('KA2', '1'))
    KPOOLM = int(os.environ.get('KPOOLM', '0'))
    KC1 = int(os.environ.get('KC1', '1'))
    KC1B = int(os.environ.get('KC1B', '1'))
    nc = bass.Bass("TRN2", target_bir_lowering=False)

    def din(name, shape, dt=F32):
        return nc.dram_tensor(name, list(shape), dt, kind="ExternalInput").ap()

    def dint(name, shape, dt):
        return nc.dram_tensor(name, list(shape), dt, kind="Internal").ap()

    x = din("x", [S, D])
    wsrc = {
        "wg1": din("wg1", [D, DFF]), "wu1": din("wu1", [D, DFF]), "wd1": din("wd1", [DFF, D]),
        "win": din("win", [D, 7168]), "wpa": din("wpa", [512, D]), "wpf": din("wpf", [512, D]),
        "wo": din("wo", [D, D]),
        "wg2": din("wg2", [D, DFF]), "wu2": din("wu2", [D, DFF]), "wd2": din("wd2", [DFF, D]),
    }
    lnp = din("lnp", [128, 6, D])
    bqk = din("bqk", [128, 24])
    bgt = din("bgt", [128, 16])
    bvu = din("bvu", [128, 2048])
    ebias = din("ebias", [128, 12, 384])
    c1tw = din("c1tw", [128, 64, 256], BF16)
    r2 = din("r2", [64, 128], BF16)
    cds = din("cds", [128, 256], BF16)
    identd = din("identd", [128, 128], BF16)
    out = nc.dram_tensor("out", [NOWN, D], F32, kind="ExternalOutput").ap()

    wb = {k: dint(k + "_b", v.shape, BF16) for k, v in wsrc.items()}
    s_h1 = dint("s_h1", [NOWN, D], F32)
    s_qT = dint("s_qT", [12, 128, NOWN], BF16)
    s_kT = dint("s_kT", [12, 128, NKV], BF16)
    s_v = dint("s_v", [PAD + NKV, 1536], BF16)
    s_u = dint("s_u", [S, 512], BF16)
    s_gT = dint("s_gT", [16, 128, NOWN], BF16)
    s_B = dint("s_B", [128, 64, 2, 512], BF16)
    s_at = dint("s_at", [4, 128, NOWN], BF16)
    s_ft = dint("s_ft", [4, 128, NOWN], BF16)

    es = contextlib.ExitStack()
    with es:
        kx = Kx(nc, es)

        def sb(name, shape, dt, stack=es):
            return stack.enter_context(nc.sbuf_tensor(name, list(shape), dt))

        NB = 6
        banks = [Tl(es.enter_context(nc.psum_tensor("pb%d" % i, [128, 512], F32))) for i in range(NB)]
        ptrs = [Tl(es.enter_context(nc.psum_tensor("ptr%d" % i, [128, 1024], BF16))[:, 0:512]) for i in range(2)]
        rot = {"b": 0, "t": 0}

        def nbank():
            b = banks[rot["b"] % NB]
            rot["b"] += 1
            return b

        def ntr():
            b = ptrs[rot["t"] % 2]
            rot["t"] += 1
            return b

        T_wb = {k: Tl(v) for k, v in wb.items()}
        T_h1, T_qT, T_kT, T_v, T_u, T_gT, T_B, T_at, T_ft = (Tl(a) for a in (s_h1, s_qT, s_kT, s_v, s_u, s_gT, s_B, s_at, s_ft))

        ident = Tl(sb("ident", [128, 128], BF16))
        lnp_t = Tl(sb("lnp_t", [128, 6, D], F32))
        bqk_t = Tl(sb("bqk_t", [128, 24], F32))
        bgt_t = Tl(sb("bgt_t", [128, 16], F32))
        epsc = Tl(sb("epsc", [128, 3], F32))
        zer = Tl(sb("zer", [128, 1536], BF16))
        kx.dma("sp", "d:c", ident.ap[:], identd, writes=[ident])
        kx.dma("sp", "d:c", lnp_t.ap[:], lnp, writes=[lnp_t])
        kx.dma("sp", "d:c", bqk_t.ap[:], bqk, writes=[bqk_t])
        kx.dma("sp", "d:c", bgt_t.ap[:], bgt, writes=[bgt_t])
        kx.op("dve", lambda e: e.memset(epsc.ap[:, 0:1], EPS / ALPHA ** 2), writes=[epsc])
        kx.op("dve", lambda e: e.memset(zer.ap[:], 0.0), writes=[zer])
        for i in range(PAD // 128):
            kx.dma("pool", "d:st", s_v[i * 128:(i + 1) * 128, :], zer.ap[:], reads=[zer], writes=[T_v], disjoint=True)

        Twb = {}
        for nm_, nb_ in (("wg1", 6), ("wu1", 6), ("wd1", 6), ("win", 14), ("wpa", 2), ("wpf", 2), ("wo", 2),
                         ("wg2", 6), ("wu2", 6), ("wd2", 6)):
            for b_ in range(nb_):
                Twb[(nm_, b_)] = Tl(wb[nm_])

        def twb(name, blk):
            return Twb[(name, blk)]

        def cast_region(name, r0, r1, c0, c1, blk):
            t = Twb[(name, blk)]
            rstep = 256 if (c1 - c0) <= 512 else 128
            ra = r0
            while ra < r1:
                rb = min(r1, ra + rstep)
                ca = c0
                while ca < c1:
                    cb_ = min(c1, ca + 1024)
                    kx.dma("pool", "d:cc", wb[name][ra:rb, ca:cb_], wsrc[name][ra:rb, ca:cb_], writes=[t], disjoint=True, st=t)
                    ca = cb_
                ra = rb

        def cast_ffn(wg, wu, wd):
            for cb in range(6):
                c0 = cb * 512
                w = min(512, DFF - c0)
                cast_region(wg, 0, D, c0, c0 + w, cb)
                cast_region(wu, 0, D, c0, c0 + w, cb)
            for fb in range(6):
                r0 = fb * 512
                cast_region(wd, r0, min(DFF, r0 + 512), 0, D, fb)

        def cast_cols(name, nblk):
            rows = wsrc[name].shape[0]
            for cb in range(nblk):
                cast_region(name, 0, rows, cb * 512, cb * 512 + 512, cb)

        cast_ffn("wg1", "wu1", "wd1")

        def cast_piece(i):
            if i == 0:
                cast_cols("win", 14)
            elif i == 1:
                cast_cols("wpa", 2)
                cast_cols("wpf", 2)
                cast_cols("wo", 2)
            elif 2 <= i <= 7:
                cb = i - 2
                c0 = cb * 512
                w = min(512, DFF - c0)
                cast_region("wg2", 0, D, c0, c0 + w, cb)
                cast_region("wu2", 0, D, c0, c0 + w, cb)
                r0 = cb * 512
                cast_region("wd2", r0, min(DFF, r0 + 512), 0, D, cb)

        rings = {}
        cnt = {"stg": 0, "sg": 0, "ev": 0, "tmp": 0, "t12": 0}
        T_out = Tl(out)

        class Tok:
            pass

        def mk_tok(stack, key, nxf, p1):
            def al(name, shape, dt):
                if kx.dry:
                    return Tl(_Any())
                return Tl(sb(key + name, shape, dt, stack))
            t = Tok()
            t.wslots = [al("wsl%d" % i, [128, 4096], BF16) for i in range(5)]
            if key not in rings:
                rings[key] = WRing(kx, t.wslots)
            rings[key].slots = t.wslots
            t.ring = rings[key]
            t.xf = [[al("xf%d_%d" % (b_, j), [128, D], F32) for j in range(4)] for b_ in range(nxf)]
            t.xb = [al("xb%d" % j, [128, D], BF16) for j in range(4)]
            t.xT = [al("xT%d" % k, [128, CH], BF16) for k in range(8)]
            t.hT = [al("hT%d" % f, [128, CH], BF16) for f in range(NFC)]
            t.sg = [al("sg%d" % i, [128, CH], F32) for i in range(2)]
            t.tmpA = [al("tmpA%d" % i, [128, D], F32) for i in range(2)]
            t.stats = []
            for j in range(4):
                if kx.dry:
                    t.stats.append((Tl(_Any()), Tl(_Any()), _Any()))
                else:
                    st_ = sb(key + "stats%d" % j, [128, 2, 6], F32, stack)
                    t.stats.append((Tl(st_[:, 0, :]), Tl(st_[:, 1, :]), st_))
            t.mv = al("mv", [128, 4, 2], F32)
            t.lnw = al("lnw", [128, 12], F32)
            if p1:
                t.stg = [al("stg%d" % i, [128, 4, CH], BF16) for i in range(3)]
                t.bvu = al("bvu_t", [128, 2048], F32)
                t.h1b = [al("h1b%d" % j, [128, D], BF16) for j in range(4)]
                t.h1T = [al("h1T%d" % k, [128, CH], BF16) for k in range(8)]
            return t

        def evac_copy(dst, dst_ap, src, src_ap):
            cnt["ev"] += 1
            if cnt["ev"] % 2:
                kx.op("dve", lambda e: e.tensor_copy(out=dst_ap, in_=src_ap), reads=[src], writes=[dst])
            else:
                kx.op("act", lambda e: e.activation(out=dst_ap, in_=src_ap, func=AF.Copy), reads=[src], writes=[dst])

        def to_fm(srcb, dstT):
            for dc in range(8):
                p = ntr()
                for j in range(4):
                    kx.op("pe", lambda e: e.transpose(out=p.ap[:, j * 128:(j + 1) * 128],
                                                      in_=srcb[j].ap[:, dc * 128:(dc + 1) * 128], identity=ident.ap[:]),
                          reads=[srcb[j], ident], writes=[p] if j == 0 else [], signal=(j == 3))
                evac_copy(dstT[dc], dstT[dc].ap[:], p, p.ap[:])

        def wblock_cols(name, c0, w, kchunks=8):
            src = wb[name][:, c0:c0 + w].rearrange("(k p) n -> p k n", p=128)
            return [(lambda a: a[:, 0:kchunks * w].rearrange("p (k n) -> p k n", k=kchunks), src, twb(name, c0 // 512))]

        def ffn(tk, X, wg, wu, wd, lni, cres, after_ln):
            ffn_gu(tk, wg, wu)
            ffn_down(tk, X, wd, lni, cres, after_ln)

        def ffn_gu(tk, wg, wu):
            ring, xT, hT, sg = tk.ring, tk.xT, tk.hT, tk.sg
            for cb in range(6):
                c0 = cb * 512
                w = min(512, DFF - c0)
                bg = ring.next(wblock_cols(wg, c0, w))
                bu = ring.next(wblock_cols(wu, c0, w))
                nf = (w + 127) // 128
                for fl in range(nf):
                    f = cb * 4 + fl
                    m = min(128, w - fl * 128)
                    pg, pu = nbank(), nbank()
                    for (pt, blk) in ((pg, bg), (pu, bu)):
                        for kc in range(8):
                            kx.op("pe", lambda e: e.matmul(
                                pt.ap[0:m, :], lhsT=blk.ap[:, 0:8 * w].rearrange("p (k n) -> p k n", k=8)[:, kc, fl * 128:fl * 128 + m],
                                rhs=xT[kc].ap[:], start=(kc == 0), stop=(kc == 7)),
                                reads=[blk, xT[kc]], writes=[pt] if kc == 0 else [], signal=(kc == 7))
                    s_ = sg[cnt["sg"] % 2]
                    cnt["sg"] += 1
                    kx.op("act", lambda e: e.activation(out=s_.ap[0:m, :], in_=pg.ap[0:m, :], func=AF.Silu), reads=[pg], writes=[s_])
                    kx.op("dve", lambda e: e.tensor_tensor(out=hT[f].ap[0:m, :], in0=s_.ap[0:m, :], in1=pu.ap[0:m, :], op=ALU.mult),
                          reads=[s_, pu], writes=[hT[f]])
        def ffn_down(tk, X, wd, lni, cres, after_ln):
            ring, hT = tk.ring, tk.hT
            for ps_ in range(2):
                acc = [[nbank(), nbank()] for _ in range(2)]
                for fb in range(6):
                    f0 = fb * 4
                    if fb < 5:
                        parts = [(lambda a: a[:, 0:4096].rearrange("p (f n) -> p f n", f=4),
                                  wb[wd][f0 * 128:(f0 + 4) * 128, :].rearrange("(f p) n -> p f n", p=128), twb(wd, fb))]
                    else:
                        parts = [(lambda a: a[:, 0:1024], wb[wd][2560:2688, :], twb(wd, fb)),
                                 (lambda a: a[0:64, 1024:2048], wb[wd][2688:2752, :], twb(wd, fb))]
                    blk = ring.next(parts)
                    nfl = 4 if fb < 5 else 2
                    for fl in range(nfl):
                        f = f0 + fl
                        kf = 128 if f < 21 else 64
                        for jj in range(2):
                            j = 2 * ps_ + jj
                            for half in range(2):
                                last = (fl == nfl - 1 and jj == 1 and half == 1)
                                kx.op("pe", lambda e: e.matmul(
                                    acc[jj][half].ap[:, :], lhsT=hT[f].ap[0:kf, j * 128:(j + 1) * 128],
                                    rhs=blk.ap[0:kf, fl * 1024 + half * 512: fl * 1024 + half * 512 + 512],
                                    start=(f == 0), stop=(f == NFC - 1)),
                                    reads=[blk, hT[f]], writes=[acc[jj][half]] if f in (0, NFC - 1) else [], signal=(f == NFC - 1 or last))
                for jj in range(2):
                    j = 2 * ps_ + jj
                    for half in range(2):
                        hs = slice(half * 512, half * 512 + 512)
                        kx.op("dve", lambda e: e.scalar_tensor_tensor(
                            out=X[j].ap[:, hs], in0=acc[jj][half].ap[:, :], scalar=cres, in1=X[j].ap[:, hs],
                            op0=ALU.mult, op1=ALU.add), reads=[acc[jj][half], X[j]], writes=[X[j]])
                    ln_stats(tk, X, j)
            ln_finish(tk, X, lni, after_ln)

        def ln_stats(tk, X, j):
            s0, s1, sfull = tk.stats[j]
            kx.op("dve", lambda e: e.bn_stats(out=s0.ap, in_=X[j].ap[:, 0:512]), reads=[X[j]], writes=[s0])
            kx.op("dve", lambda e: e.bn_stats(out=s1.ap, in_=X[j].ap[:, 512:1024]), reads=[X[j]], writes=[s1])
            kx.op("dve", lambda e: e.bn_aggr(out=tk.mv.ap[:, j, :], in_=sfull[:]), reads=[s0, s1], writes=[tk.mv])

        def ln_finish(tk, X, lni, after_ln):
            mv, lnw = tk.mv, tk.lnw
            kx.op("act", lambda e: e.activation(out=lnw.ap[:, 0:4], in_=mv.ap[:, :, 1], func=AF.Sqrt, bias=epsc.ap[:, 0:1], scale=1.0),
                  reads=[mv, epsc], writes=[lnw])
            kx.op("dve", lambda e: e.reciprocal(out=lnw.ap[:, 4:8], in_=lnw.ap[:, 0:4]), reads=[lnw], writes=[lnw])
            kx.op("dve", lambda e: e.scalar_tensor_tensor(out=lnw.ap[:, 8:12], in0=mv.ap[:, :, 0], scalar=-1.0, in1=lnw.ap[:, 4:8],
                                                          op0=ALU.mult, op1=ALU.mult), reads=[mv, lnw], writes=[lnw])
            for j in range(4):
                t = tk.tmpA[cnt["tmp"] % 2]
                cnt["tmp"] += 1
                kx.op("act", lambda e: e.activation(out=t.ap[:], in_=X[j].ap[:], func=AF.Identity,
                                                    bias=lnw.ap[:, 8 + j:9 + j], scale=lnw.ap[:, 4 + j:5 + j]),
                      reads=[X[j], lnw], writes=[t])
                kx.op("dve", lambda e: e.tensor_tensor(out=t.ap[:], in0=t.ap[:], in1=lnp_t.ap[:, 2 * lni, :], op=ALU.mult),
                      reads=[t, lnp_t], writes=[t])
                kx.op("dve", lambda e: e.tensor_tensor(out=X[j].ap[:], in0=t.ap[:], in1=lnp_t.ap[:, 2 * lni + 1, :], op=ALU.add),
                      reads=[t, lnp_t], writes=[X[j]])
                after_ln(j)

        def phase1(tk):
            xf, xb, xT, ring = tk.xf, tk.xb, tk.xT, tk.ring
            kx.dma("sp", "d:c", tk.bvu.ap[:], bvu, writes=[tk.bvu])

            def load_x(ci, buf):
                for j in range(4):
                    kx.dma("sp", "d:x", xf[buf][j].ap[:], x[ci * CH + j * 128:ci * CH + (j + 1) * 128, :], writes=[xf[buf][j]])

            def nstg():
                t = tk.stg[cnt["stg"] % 3]
                cnt["stg"] += 1
                return t

            def w_in_stage(wci):
                own = wci < 8
                kv = wci < 10
                t0 = wci * CH
                xT = tk.h1T
                for blk_i in range(14):
                    kind = "q" if blk_i < 3 else "k" if blk_i < 6 else "v" if blk_i < 9 else "u" if blk_i == 9 else "g"
                    if kind in ("q", "g") and not own:
                        continue
                    if kind in ("k", "v") and not kv:
                        continue
                    if wci == 9 and blk_i in (3, 4, 6, 7):
                        continue
                    blk = ring.next(wblock_cols("win", blk_i * 512, 512))
                    bview = lambda b_: b_.ap[:, 0:4096].rearrange("p (k n) -> p k n", k=8)
                    sgt = nstg()
                    if kind in ("q", "k", "g"):
                        for hh in range(4):
                            p = nbank()
                            for kc in range(8):
                                kx.op("pe", lambda e: e.matmul(p.ap[:, :], lhsT=bview(blk)[:, kc, hh * 128:(hh + 1) * 128],
                                                               rhs=xT[kc].ap[:], start=(kc == 0), stop=(kc == 7)),
                                      reads=[blk, xT[kc]], writes=[p] if kc == 0 else [], signal=(kc == 7))
                            if kind == "g":
                                col = (blk_i - 10) * 4 + hh
                                kx.op("act", lambda e: e.activation(out=sgt.ap[:, hh, :], in_=p.ap[:, :], func=AF.Sigmoid,
                                                                    bias=bgt_t.ap[:, col:col + 1], scale=1.0),
                                      reads=[p, bgt_t], writes=[sgt])
                            else:
                                col = blk_i * 4 + hh
                                kx.op("act", lambda e: e.activation(out=sgt.ap[:, hh, :], in_=p.ap[:, :], func=AF.Identity,
                                                                    bias=bqk_t.ap[:, col:col + 1], scale=1.0),
                                      reads=[p, bqk_t], writes=[sgt])
                        if kind == "q":
                            dst, Tt = s_qT[blk_i * 4:blk_i * 4 + 4, :, t0:t0 + CH], T_qT
                        elif kind == "k":
                            dst, Tt = s_kT[(blk_i - 3) * 4:(blk_i - 3) * 4 + 4, :, t0:t0 + CH], T_kT
                        else:
                            dst, Tt = s_gT[(blk_i - 10) * 4:(blk_i - 10) * 4 + 4, :, t0:t0 + CH], T_gT
                        kx.dma("pool", "d:st", dst.rearrange("h p t -> p h t"), sgt.ap[:], reads=[sgt], writes=[Tt], disjoint=True)
                    else:
                        boff = (blk_i - 6) * 512
                        for j in range(4):
                            p = nbank()
                            for kc in range(8):
                                kx.op("pe", lambda e: e.matmul(p.ap[:, :], lhsT=xT[kc].ap[:, j * 128:(j + 1) * 128],
                                                               rhs=bview(blk)[:, kc, :], start=(kc == 0), stop=(kc == 7)),
                                      reads=[blk, xT[kc]], writes=[p] if kc == 0 else [], signal=(kc == 7))
                            kx.op("dve", lambda e: e.tensor_tensor(out=sgt.ap[:, j, :], in0=p.ap[:, :], in1=tk.bvu.ap[:, boff:boff + 512], op=ALU.add),
                                  reads=[p, tk.bvu], writes=[sgt])
                        if kind == "v":
                            dst = s_v[PAD + t0:PAD + t0 + CH, boff:boff + 512].rearrange("(j p) c -> p j c", p=128)
                            Tt = T_v
                        else:
                            dst = s_u[t0:t0 + CH, :].rearrange("(j p) c -> p j c", p=128)
                            Tt = T_u
                        kx.dma("pool", "d:st", dst, sgt.ap[:], reads=[sgt], writes=[Tt], disjoint=True)

            load_x(0, 0)
            for ci in range(16):
                buf = ci % 2
                own = ci < 8
                kv = ci < 10
                t0 = ci * CH
                if ci + 1 < 16:
                    load_x(ci + 1, 1 - buf)
                X = xf[buf]
                for j in range(4):
                    kx.op("act", lambda e: e.activation(out=xb[j].ap[:], in_=X[j].ap[:], func=AF.Copy), reads=[X[j]], writes=[xb[j]])
                to_fm(xb, xT)

                def after_ln1(j):
                    if own:
                        kx.dma("pool", "d:st", s_h1[t0 + j * 128:t0 + (j + 1) * 128, :], X[j].ap[:], reads=[X[j]], writes=[T_h1], disjoint=True)
                    kx.op("act", lambda e: e.activation(out=tk.h1b[j].ap[:], in_=X[j].ap[:], func=AF.Copy), reads=[X[j]], writes=[tk.h1b[j]])

                ffn(tk, X, "wg1", "wu1", "wd1", 0, 0.5 / ALPHA, after_ln1)
                if ci == 0:
                    cast_piece(0)
                elif 8 <= ci <= 14:
                    cast_piece(ci - 7)
                if ci > 0:
                    w_in_stage(ci - 1)
                to_fm(tk.h1b, tk.h1T)
            w_in_stage(15)

        def phase3(tk, stack):
            def al(name, shape, dt):
                if kx.dry:
                    return Tl(_Any())
                return Tl(sb("p3" + name, shape, dt, stack))
            ga = al("ga", [128, 8, CH], BF16)
            gf = al("gf", [128, 8, CH], BF16)
            at = al("at", [128, 4, CH], BF16)
            ft = al("ft", [128, 4, CH], BF16)
            mT = [al("mT%d" % k, [128, CH], BF16) for k in range(8)]
            t12 = [al("t12_%d" % i, [128, CH], F32) for i in range(4)]
            xf, xb, xT, ring = tk.xf, tk.xb, tk.xT, tk.ring
            def load_h1(ci_):
                t0_ = ci_ * CH
                for j in range(4):
                    kx.dma("sp", "d:x", xf[(ci_ % 2) * KC1][j].ap[:], s_h1[t0_ + j * 128:t0_ + (j + 1) * 128, :], reads=[T_h1], writes=[xf[(ci_ % 2) * KC1][j]])

            def load_aux(ci_):
                t0_ = ci_ * CH
                kx.dma("sp", "d:l2", at.ap[:], s_at[:, :, t0_:t0_ + CH].rearrange("h p t -> p h t"), reads=[T_at], writes=[at])
                kx.dma("sp", "d:l2", ft.ap[:], s_ft[:, :, t0_:t0_ + CH].rearrange("h p t -> p h t"), reads=[T_ft], writes=[ft])
                kx.dma("sp", "d:l2", ga.ap[:], s_gT[0:8, :, t0_:t0_ + CH].rearrange("h p t -> p h t"), reads=[T_gT], writes=[ga])
                kx.dma("sp", "d:l2", gf.ap[:], s_gT[8:16, :, t0_:t0_ + CH].rearrange("h p t -> p h t"), reads=[T_gT], writes=[gf])

            def stage_A(ci):
                t0 = ci * CH
                X = xf[ci % 2]
                if ci == 0:
                    load_aux(0)
                for cb in range(2):
                    ba = ring.next(wblock_cols("wpa", cb * 512, 512, 4))
                    bf_ = ring.next(wblock_cols("wpf", cb * 512, 512, 4))
                    v4 = lambda b_: b_.ap[:, 0:2048].rearrange("p (k n) -> p k n", k=4)
                    for dl in range(4):
                        dc = cb * 4 + dl
                        pa, pf = nbank(), nbank()
                        for (pt, blk, src) in ((pa, ba, at), (pf, bf_, ft)):
                            for kc in range(4):
                                kx.op("pe", lambda e: e.matmul(pt.ap[:, :], lhsT=v4(blk)[:, kc, dl * 128:(dl + 1) * 128],
                                                               rhs=src.ap[:, kc, :], start=(kc == 0), stop=(kc == 3)),
                                      reads=[blk, src], writes=[pt] if kc == 0 else [], signal=(kc == 3))
                        ta, tb = t12[(cnt["t12"] % 2) * 2], t12[(cnt["t12"] % 2) * 2 + 1]
                        cnt["t12"] += 1
                        kx.op("dve", lambda e: e.tensor_tensor(out=ta.ap[:], in0=pa.ap[:, :], in1=ga.ap[:, dc, :], op=ALU.mult), reads=[pa, ga], writes=[ta])
                        kx.op("dve", lambda e: e.tensor_tensor(out=tb.ap[:], in0=pf.ap[:, :], in1=gf.ap[:, dc, :], op=ALU.mult), reads=[pf, gf], writes=[tb])
                        kx.op("pool", lambda e: e.tensor_tensor(out=mT[dc].ap[:], in0=ta.ap[:], in1=tb.ap[:], op=ALU.add), reads=[ta, tb], writes=[mT[dc]])
                if ci + 1 < 8:
                    load_aux(ci + 1)
                for half in range(2):
                    blk = ring.next(wblock_cols("wo", half * 512, 512))
                    bview = lambda b_: b_.ap[:, 0:4096].rearrange("p (k n) -> p k n", k=8)
                    for j in range(4):
                        p = nbank()
                        for kc in range(8):
                            kx.op("pe", lambda e: e.matmul(p.ap[:, :], lhsT=mT[kc].ap[:, j * 128:(j + 1) * 128], rhs=bview(blk)[:, kc, :],
                                                           start=(kc == 0), stop=(kc == 7)),
                                  reads=[blk, mT[kc]], writes=[p] if kc == 0 else [], signal=(kc == 7))
                        hs = slice(half * 512, half * 512 + 512)
                        kx.op("dve", lambda e: e.scalar_tensor_tensor(out=X[j].ap[:, hs], in0=p.ap[:, :], scalar=1.0 / ALPHA, in1=X[j].ap[:, hs],
                                                                      op0=ALU.mult, op1=ALU.add), reads=[p, X[j]], writes=[X[j]])
                for j in range(4):
                    ln_stats(tk, X, j)

                def after_ln2(j):
                    kx.op("act", lambda e: e.activation(out=xb[j].ap[:], in_=X[j].ap[:], func=AF.Copy), reads=[X[j]], writes=[xb[j]])

                ln_finish(tk, X, 1, after_ln2)

            def stage_down(ci):
                t0 = ci * CH
                X = xf[ci % 2]

                def after_ln3(j):
                    kx.dma("pool", "d:st", out[t0 + j * 128:t0 + (j + 1) * 128, :], X[j].ap[:], reads=[X[j]], writes=[T_out], disjoint=True)

                ffn_down(tk, X, "wd2", 2, 0.5 / ALPHA, after_ln3)
                if ci + 2 < 8:
                    load_h1(ci + 2)

            load_h1(0)
            load_h1(1)
            stage_A(0)
            for ci in range(8):
                to_fm(xb, xT)
                ffn_gu(tk, "wg2", "wu2")
                if ci + 1 < 8:
                    stage_A(ci + 1)
                stage_down(ci)

        def phase2a(st):
            kbuf = [Tl(sb("kbuf%d" % i, [128, PAD + NKV], BF16, st)) for i in range(2)]
            qbuf = [Tl(sb("qbuf%d" % i, [128, NOWN], BF16, st)) for i in range(2)]
            vbuf = [Tl(sb("vbuf%d" % i, [128, 33, 128], BF16, st)) for i in range(2)]
            eb = Tl(sb("eb", [128, 12, 384], F32, st))
            ones = Tl(sb("ones", [128, 128], BF16, st))
            accn = Tl(sb("accn", [128, NOWN], F32, st))
            accd = Tl(sb("accd", [128, NOWN], F32, st))
            exs = [Tl(sb("exs%d" % i, [128, 256], F32, st)) for i in range(3)]
            pTs = [Tl(sb("pTs%d" % i, [128, 256], BF16, st)) for i in range(4)]
            rec = [Tl(sb("rec%d" % i, [128, CH], F32, st)) for i in range(2)]
            ast = [Tl(sb("ast%d" % i, [128, CH], BF16, st)) for i in range(2)]
            kx.dma("sp", "d:c", eb.ap[:], ebias, writes=[eb])
            ebh = Tl(sb("ebh", [128, 12, 384], BF16, st))
            ebl = Tl(sb("ebl", [128, 12, 384], BF16, st))
            ebt = Tl(sb("ebt", [128, 384], F32, st))
            for h_ in range(12):
                kx.op("dve", lambda e: e.tensor_scalar_mul(out=eb.ap[:, h_, :], in0=eb.ap[:, h_, :], scalar1=float(128.0 ** 0.5)), reads=[eb], writes=[eb])
                kx.op("dve", lambda e: e.tensor_copy(out=ebh.ap[:, h_, :], in_=eb.ap[:, h_, :]), reads=[eb], writes=[ebh])
                kx.op("dve", lambda e: e.tensor_copy(out=ebt.ap[:], in_=ebh.ap[:, h_, :]), reads=[ebh], writes=[ebt])
                kx.op("dve", lambda e: e.tensor_tensor(out=ebl.ap[:, h_, :], in0=eb.ap[:, h_, :], in1=ebt.ap[:], op=ALU.subtract), reads=[eb, ebt], writes=[ebl])
            kx.op("dve", lambda e: e.memset(ones.ap[:], 1.0), writes=[ones])
            for i in range(2):
                kx.op("dve", lambda e, i=i: e.memset(kbuf[i].ap[:, 0:PAD], 0.0), writes=[kbuf[i]])
            Sb = banks[0:2]
            Nb = banks[2:4]
            Db = banks[4:6]
            c = {"s": 0, "ex": 0, "pt": 0, "o": 0}
            scale = 128.0 ** -0.5
            heads = [(hp, g) for hp in range(4) for g in range(3)]
            jobs = [(hi, r) for hi, (hp, g) in enumerate(heads) for r in range(DILS[g])]

            def load_head(hi):
                hp, g = heads[hi]
                h = 4 * g + hp
                kb, qb = kbuf[hi % 2], qbuf[hi % 2]
                kx.dma("sp", "d:x", kb.ap[:, PAD:PAD + NKV], s_kT[h, :, :], reads=[T_kT], writes=[kb])
                kx.dma("sp", "d:x", qb.ap[:], s_qT[h, :, :], reads=[T_qT], writes=[qb])

            def load_v(ji):
                hi, r = jobs[ji]
                hp, g = heads[hi]
                h = 4 * g + hp
                d = DILS[g]
                vb = vbuf[ji % 2]
                row0 = PAD + r - 64 * d
                nch = NOWN // d // 128 + 1
                for cc in range(nch):
                    rs = row0 + d * 128 * cc
                    kx.dma("sp", "d:l2", vb.ap[:, cc, :], s_v[ss(rs, 128, d), h * 128:(h + 1) * 128],
                           reads=[T_v], writes=[vb], disjoint=(cc > 0))

            load_head(0)
            load_v(0)
            accst = {"n": True, "d": True}
            for ji, (hi, r) in enumerate(jobs):
                hp, g = heads[hi]
                h = 4 * g + hp
                d = DILS[g]
                kb, qb, vb = kbuf[hi % 2], qbuf[hi % 2], vbuf[ji % 2]
                if r == 0:
                    accst["n"] = True
                    accst["d"] = True
                    if hi + 1 < len(heads):
                        load_head(hi + 1)
                if ji + 1 < len(jobs):
                    load_v(ji + 1)
                nt = NOWN // d // 128
                nch = nt + 1

                def stage1(cc):
                    tl_lo = cc - 1 if cc > 0 else None
                    tl_hi = cc if cc < nt else None
                    q0 = 128 * (cc - 1) if cc > 0 else 0
                    nq = (128 if tl_lo is not None else 0) + (128 if tl_hi is not None else 0)
                    kc0 = PAD + r + d * (-64 + 128 * cc)
                    qc0 = r + d * q0
                    sbk = Sb[c["s"] % 2]
                    c["s"] += 1
                    if cc == 0:
                        tsl = slice(256, 384)
                    elif cc == nt:
                        tsl = slice(0, 128)
                    else:
                        tsl = slice(0, 256)
                    kx.op("pe", lambda e: e.matmul(sbk.ap[:, 0:nq], lhsT=kb.ap[:, ss(kc0, 128, d)], rhs=qb.ap[:, ss(qc0, nq, d)], start=True, stop=False),
                          reads=[kb, qb], writes=[sbk], signal=False)
                    kx.op("pe", lambda e: e.matmul(sbk.ap[:, 0:nq], lhsT=ident.ap[:], rhs=ebh.ap[:, h, tsl], start=False, stop=False),
                          reads=[ident, ebh], writes=[], signal=False)
                    kx.op("pe", lambda e: e.matmul(sbk.ap[:, 0:nq], lhsT=ident.ap[:], rhs=ebl.ap[:, h, tsl], start=False, stop=True),
                          reads=[ident, ebl], writes=[sbk])
                    pT = pTs[c["pt"] % 4]
                    c["pt"] += 1
                    kx.op("act", lambda e: e.activation(out=pT.ap[:, 0:nq], in_=sbk.ap[:, 0:nq], func=AF.Exp, scale=scale),
                          reads=[sbk], writes=[pT])
                    return (cc, tl_lo, tl_hi, pT)

                def stage2(st_):
                    cc, tl_lo, tl_hi, pT = st_
                    col = 0
                    for (tl, first) in ((tl_lo, False), (tl_hi, True)):
                        if tl is None:
                            continue
                        nbk, dbk = Nb[tl % 2], Db[tl % 2]
                        kx.op("pe", lambda e: e.matmul(nbk.ap[:, 0:128], lhsT=vb.ap[:, cc, :], rhs=pT.ap[:, col:col + 128], start=first, stop=(not first)),
                              reads=[vb, pT], writes=[nbk], signal=(not first))
                        kx.op("pe", lambda e: e.matmul(dbk.ap[:, 0:128], lhsT=ones.ap[:], rhs=pT.ap[:, col:col + 128], start=first, stop=(not first)),
                              reads=[ones, pT], writes=[dbk], signal=(not first))
                        if not first:
                            dsl = ss(r + d * 128 * tl, 128, d)
                            dn, dd_ = (KDISJ and not accst["n"]), (KDISJ and not accst["d"])
                            accst["n"] = False
                            accst["d"] = False
                            if g == 0:
                                kx.op("act", lambda e: e.activation(out=accn.ap[:, dsl], in_=nbk.ap[:, 0:128], func=AF.Copy),
                                      reads=[nbk], writes=[accn], disjoint=dn)
                                kx.op("dve", lambda e: e.tensor_copy(out=accd.ap[:, dsl], in_=dbk.ap[:, 0:128]),
                                      reads=[dbk], writes=[accd], disjoint=dd_)
                            else:
                                kx.op("dve", lambda e: e.tensor_tensor(out=accn.ap[:, dsl], in0=nbk.ap[:, 0:128], in1=accn.ap[:, dsl], op=ALU.add),
                                      reads=[nbk], writes=[accn], disjoint=dn)
                                kx.op("dve", lambda e: e.tensor_tensor(out=accd.ap[:, dsl], in0=dbk.ap[:, 0:128], in1=accd.ap[:, dsl], op=ALU.add),
                                      reads=[dbk], writes=[accd], disjoint=dd_)
                        col += 128

                prev = None
                for cc in range(nch):
                    cur = stage1(cc)
                    if not KPIPE:
                        stage2(cur)
                        continue
                    if prev is not None:
                        stage2(prev)
                    prev = cur
                if KPIPE:
                    stage2(prev)
                if not (g == 2 and r == d - 1):
                    continue
                for kb_ in range(8):
                    cs = slice(kb_ * CH, (kb_ + 1) * CH)
                    rc = rec[c["o"] % 2]
                    a_ = ast[c["o"] % 2]
                    c["o"] += 1
                    kx.op("dve", lambda e, rc=rc, cs=cs: e.reciprocal(out=rc.ap[:], in_=accd.ap[:, cs]), reads=[accd], writes=[rc])
                    kx.op("pool", lambda e, rc=rc, cs=cs, a_=a_: e.tensor_tensor(out=a_.ap[:], in0=accn.ap[:, cs], in1=rc.ap[:], op=ALU.mult),
                          reads=[accn, rc], writes=[a_])
                    kx.dma("pool", "d:st", s_at[hp, :, cs], a_.ap[:], reads=[a_], writes=[T_at], disjoint=True)

        def phase2b(st):
            r2t = Tl(sb("r2t", [64, 128], BF16, st))
            cd = Tl(sb("cd", [128, 256], BF16, st))
            for t, s_ in ((r2t, r2), (cd, cds)):
                kx.dma("sp", "d:c", t.ap[:], s_, writes=[t])
            ub = [Tl(sb("ub%d" % i, [128, 8, 512], BF16, st)) for i in range(2)]
            tbk = [Tl(sb("tbk%d" % i, [128, 8, 256], BF16, st)) for i in range(2)]
            stB = [Tl(sb("stB%d" % i, [128, 4, 2, 512], BF16, st)) for i in range(2)]
            Bt = [Tl(sb("Bt%d" % i, [64, 8, 2, 512], BF16, st)) for i in range(2)]
            XT = [Tl(sb("XT%d" % g, [128, 2, NOWN], BF16, st)) for g in range(4)]
            fst = [Tl(sb("fst%d" % i, [128, CH], BF16, st)) for i in range(2)]
            su_v = s_u.rearrange("(a b) c -> a b c", b=64)
            for sb8 in range(8):
                u_ = ub[sb8 % 2]
                tb_ = tbk[sb8 % 2]
                kx.dma("sp", "d:x", u_.ap[:], su_v[:, sb8 * 8:(sb8 + 1) * 8, :], reads=[T_u], writes=[u_])
                kx.dma("sp", "d:x", tb_.ap[:], c1tw[:, sb8 * 8:(sb8 + 1) * 8, :], writes=[tb_])
                for q4 in range(2):
                    sB = stB[(sb8 * 2 + q4) % 2]
                    for sl in range(4):
                        s8 = q4 * 4 + sl
                        pr, pi = nbank(), nbank()
                        kx.op("pe", lambda e: e.matmul(pr.ap[:, :], lhsT=tb_.ap[:, s8, 0:128], rhs=u_.ap[:, s8, :], start=True, stop=True),
                              reads=[tb_, u_], writes=[pr])
                        kx.op("pe", lambda e: e.matmul(pi.ap[:, :], lhsT=tb_.ap[:, s8, 128:256], rhs=u_.ap[:, s8, :], start=True, stop=True),
                              reads=[tb_, u_], writes=[pi])
                        kx.op("dve", lambda e: e.tensor_copy(out=sB.ap[:, sl, 0, :], in_=pr.ap[:, :]), reads=[pr], writes=[sB])
                        kx.op("act", lambda e: e.activation(out=sB.ap[:, sl, 1, :], in_=pi.ap[:, :], func=AF.Copy), reads=[pi], writes=[sB])
                    s20 = sb8 * 8 + q4 * 4
                    kx.dma("pool", "d:st", s_B[:, s20:s20 + 4, :, :], sB.ap[:], reads=[sB], writes=[T_B], disjoint=True)
            for kb in range(16):
                bt = Bt[kb % 2]
                kx.dma("sp", "d:x", bt.ap[:], s_B[kb * 8:(kb + 1) * 8, :, :, :].rearrange("k s r c -> s k r c"), reads=[T_B], writes=[bt])
                for g in range(4):
                    p = nbank()
                    for kl in range(8):
                        kx.op("pe", lambda e, kl=kl, p=p: e.matmul(p.ap[:, kl * 64:(kl + 1) * 64], lhsT=bt.ap[:, kl, 0, g * 128:(g + 1) * 128],
                                                                    rhs=r2t.ap[:, 0:64], start=True, stop=False),
                              reads=[bt, r2t], writes=[p] if kl == 0 else [], signal=False)
                        kx.op("pe", lambda e, kl=kl, p=p: e.matmul(p.ap[:, kl * 64:(kl + 1) * 64], lhsT=bt.ap[:, kl, 1, g * 128:(g + 1) * 128],
                                                                    rhs=r2t.ap[:, 64:128], start=False, stop=True),
                              reads=[bt, r2t], writes=[], signal=(kl == 7))
                    src = p.ap[:, :].rearrange("p (k r j) -> p k r j", k=8, r=2)
                    dst = XT[g].ap[:, :, :].rearrange("p r (j k) -> p k r j", k=128)[:, kb * 8:(kb + 1) * 8, :, :]
                    evac_copy(XT[g], dst, p, src)
            for g in range(4):
                for kb_ in range(8):
                    cs = slice(kb_ * CH, (kb_ + 1) * CH)
                    p = nbank()
                    kx.op("pe", lambda e, p=p, cs=cs: e.matmul(p.ap[:, :], lhsT=cd.ap[:, 0:128], rhs=XT[g].ap[:, 0, cs], start=True, stop=False),
                          reads=[cd, XT[g]], writes=[p], signal=False)
                    kx.op("pe", lambda e, p=p, cs=cs: e.matmul(p.ap[:, :], lhsT=cd.ap[:, 128:256], rhs=XT[g].ap[:, 1, cs], start=False, stop=True),
                          reads=[cd, XT[g]], writes=[])
                    f_ = fst[(g * 8 + kb_) % 2]
                    evac_copy(f_, f_.ap[:], p, p.ap[:, :])
                    kx.dma("pool", "d:st", s_ft[g, :, cs], f_.ap[:], reads=[f_], writes=[T_ft], disjoint=True)

        kx.dry = True
        phase1(mk_tok(None, "a", 2, True))
        phase3(mk_tok(None, "c", 2, False), None)
        kx.dry = False
        rot["b"] = 0
        rot["t"] = 0
        for k_ in cnt:
            cnt[k_] = 0

        import os
        KSTOP = int(os.environ.get("KSTOP", "9"))
        if KSTOP >= 1:
          with contextlib.ExitStack() as s1:
            phase1(mk_tok(s1, "a", 2, True))
            kx.barrier()
        if KSTOP >= 2:
          with contextlib.ExitStack() as s2:
            phase2a(s2)
            kx.barrier()
        if KSTOP >= 3:
          with contextlib.ExitStack() as s2b:
            phase2b(s2b)
            kx.barrier()
        if KSTOP >= 4:
          with contextlib.ExitStack() as s3:
            phase3(mk_tok(s3, "c", 2, False), s3)
            kx.wait_all("pool", [T_out])
            kx.barrier()
    return nc


def _t5_bucket_np(rel):
    half = 16
    ret = (rel > 0).astype(np.int32) * half
    n = np.abs(rel)
    nf = np.maximum(n, 1).astype(np.float32)
    large = 8 + (np.log(nf / np.float32(8)) / np.float32(math.log(128.0)) * np.float32(8)).astype(np.int32)
    large = np.minimum(large, half - 1)
    return ret + np.where(n < 8, n, large)


def _tables(hf):
    bf = ml_dtypes.bfloat16
    fl = (lambda a, n: a) if hf == 0 else (lambda a, n: n - 1 - a)
    i1 = np.arange(128)
    s1 = fl(i1, 128)
    k1 = fl(i1, 128)
    i2 = np.arange(64)
    s2 = fl(i2, 64)
    th = 2 * np.pi * (s1[:, None, None] * k1[None, None, :] / 128.0 + s2[None, :, None] * k1[None, None, :] / 8192.0)
    c1tw = np.concatenate([np.cos(th), np.sin(th)], axis=2).astype(bf)
    j2 = np.arange(32)
    k2 = j2 if hf == 0 else 63 - j2
    th2 = 2 * np.pi * np.outer(s2, k2) / 64.0
    r2 = np.concatenate([np.cos(th2), np.sin(th2), -np.sin(th2), np.cos(th2)], axis=1).astype(bf)
    dd = np.arange(128)
    ph = 2 * np.pi * np.outer(dd, dd) / 128.0
    cds = (np.concatenate([np.cos(ph), -np.sin(ph)], axis=1) / 1024.0).astype(bf)
    return c1tw, r2, cds


def _ebias(rel_bias, hf):
    i = np.arange(128)[:, None]
    j = np.arange(256)[None, :]
    off = i - j + 64
    valid = np.abs(off) <= 64
    sign = 1 if hf == 0 else -1
    eb = np.empty((128, 12, 384), np.float32)
    for h in range(12):
        d = DILS[h // 4]
        b = rel_bias[_t5_bucket_np((off * d * sign).astype(np.int32)), h].astype(np.float32)
        ba = np.where(valid, b, np.float32(-100.0))
        first = ba[:, 128:256].copy()
        first[0:64, :] = -100.0
        eb[:, h, 0:256] = ba
        eb[:, h, 256:384] = first
    return eb


_NC_CACHE = {}


def kernel(x, ln1_g, ln1_b, ffn1_w_gate, ffn1_w_up, ffn1_w_down, w_in, b_in, rel_bias,
           w_proj_attn, w_proj_fourier, w_out, ln2_g, ln2_b, ffn2_w_gate, ffn2_w_up,
           ffn2_w_down, ln3_g, ln3_b):
    f = lambda a: np.ascontiguousarray(np.asarray(a, dtype=np.float32))
    x = f(x)
    b_in0 = f(b_in)[0]
    lnp = np.stack([f(a)[0] for a in (ln1_g, ln1_b, ln2_g, ln2_b, ln3_g, ln3_b)], axis=0)
    lnp = np.ascontiguousarray(np.broadcast_to(lnp[None], (128, 6, D)))
    bqk = np.ascontiguousarray(b_in0[0:3072].reshape(24, 128).T)
    bgt = np.ascontiguousarray(b_in0[5120:7168].reshape(16, 128).T)
    bvu = np.ascontiguousarray(np.broadcast_to(b_in0[3072:5120][None], (128, 2048)))
    rb = f(rel_bias)
    shared = {
        "wg1": f(ffn1_w_gate)[0], "wu1": f(ffn1_w_up)[0], "wd1": f(ffn1_w_down)[0],
        "win": f(w_in)[0], "wpa": f(w_proj_attn)[0], "wpf": f(w_proj_fourier)[0], "wo": f(w_out)[0],
        "wg2": f(ffn2_w_gate)[0], "wu2": f(ffn2_w_up)[0], "wd2": f(ffn2_w_down)[0],
        "lnp": lnp, "bqk": bqk, "bgt": bgt, "bvu": bvu,
        "identd": np.eye(128, dtype=np.float32).astype(ml_dtypes.bfloat16),
    }
    tabs = [_tables(0), _tables(1)]
    ebs = [_ebias(rb, 0), _ebias(rb, 1)]
    in_maps = []
    for c in range(8):
        b, hf = c // 2, c % 2
        xs = x[b] if hf == 0 else x[b, ::-1]
        c1tw, r2, cds = tabs[hf]
        m = dict(shared)
        m.update({"x": np.ascontiguousarray(xs), "ebias": ebs[hf], "c1tw": c1tw, "r2": r2, "cds": cds})
        in_maps.append(m)
    if "nc" not in _NC_CACHE:
        _NC_CACHE["nc"] = build_nc()
    nc = _NC_CACHE["nc"]
    res = run_bass_kernel_spmd(nc, in_maps, core_ids=list(range(8)))
    outp = np.empty((4, S, D), np.float32)
    for c in range(8):
        b, hf = c // 2, c % 2
        o = np.asarray(res.results[c]["out"], dtype=np.float32)
        if hf == 0:
            outp[b, 0:NOWN] = o
        else:
            outp[b, NOWN:] = o[::-1]
    return outp
```

```python
import contextlib
import math
import numpy as np
import ml_dtypes
import concourse.bass as bass
import concourse.mybir as mybir
from concourse.bass_utils import run_bass_kernel_spmd

F32 = mybir.dt.float32
BF16 = mybir.dt.bfloat16
AF = mybir.ActivationFunctionType
ALU = mybir.AluOpType

D = 1024
DFF = 2752
S = 8192
NOWN = 4096
NKV = 5120
CH = 512
PAD = 1024
ALPHA = 2.0 ** 0.25
EPS = 1e-5
DILS = (1, 4, 16)
NFC = 22


def ss(start, n, step):
    return slice(start, start + (n - 1) * step + 1, step)


class _Any:
    def __getitem__(self, k):
        return self

    def __getattr__(self, k):
        return lambda *a, **kw: self


class Tl:
    __slots__ = ("ap", "lw", "rd", "dkey")

    def __init__(self, ap):
        self.ap = ap
        self.lw = {}
        self.rd = {}
        self.dkey = None


class Kx:
    def __init__(self, nc, es):
        self.nc = nc
        self.dry = False
        self.eng = {"pe": nc.tensor, "act": nc.scalar, "dve": nc.vector, "pool": nc.gpsimd, "sp": nc.sync}
        self.sem = {}
        self.cnt = {}
        for n in ["pe", "act", "dve", "pool", "d:w", "d:x", "d:st", "d:l2", "d:c", "d:cc"]:
            self.sem[n] = es.enter_context(nc.semaphore("s_" + n.replace(":", "_")))
            self.cnt[n] = 0
        self.seen = {e: {} for e in self.eng}
        self.es = es
        self.ntile = 0
        self.free_keys = []
        self.live_keys = []

    def tile_stream(self, t, q="sp"):
        if t.dkey is None:
            t.dkey = {}
        if q not in t.dkey:
            if self.free_keys:
                k = self.free_keys.pop()
            else:
                self.ntile += 1
                k = "t:%d" % self.ntile
                self.sem[k] = self.es.enter_context(self.nc.semaphore("s_t%d" % self.ntile))
                self.cnt[k] = 0
            self.live_keys.append(k)
            t.dkey[q] = k
        return t.dkey[q]

    def _wait(self, eng, src, n):
        if eng == "pe" and src == "pe":
            return
        if self.seen[eng].get(src, 0) >= n:
            return
        self.seen[eng][src] = n
        self.eng[eng].wait_ge(self.sem[src], n * (16 if src[1] == ":" else 1))

    def _deps(self, eng, reads, writes, disjoint):
        deps = {}
        for t in reads:
            for s, n in t.lw.items():
                deps[s] = max(deps.get(s, 0), n)
        if not disjoint:
            for t in writes:
                for s, n in t.lw.items():
                    deps[s] = max(deps.get(s, 0), n)
                for s, n in t.rd.items():
                    deps[s] = max(deps.get(s, 0), n)
        for s, n in deps.items():
            self._wait(eng, s, n)

    def _stamp(self, src, n, reads, writes, disjoint):
        for t in reads:
            t.rd[src] = max(t.rd.get(src, 0), n)
        for t in writes:
            if disjoint:
                t.lw[src] = max(t.lw.get(src, 0), n)
            else:
                t.lw = {src: n}
                t.rd = {}

    def op(self, eng, fn, reads=(), writes=(), signal=True, disjoint=False):
        if self.dry:
            return
        self._deps(eng, reads, writes, disjoint)
        ins = fn(self.eng[eng])
        if signal:
            self.cnt[eng] += 1
            ins.then_inc(self.sem[eng], 1)
            n = self.cnt[eng]
        else:
            n = self.cnt[eng] + 1
        self._stamp(eng, n, reads, writes, disjoint)

    def dma(self, q, stream, out, in_, reads=(), writes=(), disjoint=False, st=None):
        if self.dry:
            return
        if st is None:
            if len(writes) == 1 and not disjoint and len(reads) <= 1 and stream != "d:cc":
                st = writes[0]
            elif disjoint and len(reads) == 1 and stream == "d:st":
                st = reads[0]
            elif stream == "d:l2" and len(writes) == 1:
                st = writes[0]
        if st is not None:
            stream = self.tile_stream(st, q)
        self._deps(q, reads, writes, disjoint)
        ins = self.eng[q].dma_start(out=out, in_=in_)
        self.cnt[stream] += 1
        ins.then_inc(self.sem[stream], 16)
        self._stamp(stream, self.cnt[stream], reads, writes, disjoint)

    def barrier(self):
        if self.dry:
            return
        for e in self.eng:
            for src in self.sem:
                if self.cnt[src] > 0:
                    self._wait(e, src, self.cnt[src])
        self.free_keys.extend(self.live_keys)
        self.live_keys = []

    def wait_all(self, eng, tiles):
        if self.dry:
            return
        self._deps(eng, tiles, (), False)


class WRing:
    def __init__(self, kx, slots):
        self.kx = kx
        self.slots = slots
        self.plan = []
        self.idx = 0
        self.issued = 0

    def reset(self):
        self.idx = 0
        self.issued = 0

    def next(self, parts):
        if self.kx.dry:
            self.plan.append(parts)
            return Tl(_Any())
        R = len(self.slots)
        while self.issued < min(self.idx + R - 1, len(self.plan)):
            j = self.issued
            sl = self.slots[j % R]
            for pi_, (dst, src, srcT) in enumerate(self.plan[j]):
                self.kx.dma("sp", "d:w", dst(sl.ap), src, reads=[srcT], writes=[sl], disjoint=(pi_ > 0), st=sl)
            self.issued += 1
        sl = self.slots[self.idx % R]
        self.idx += 1
        return sl


def build_nc():
    import os
    KPIPE = int(os.environ.get('KPIPE', '2'))
    KDISJ = int(os.environ.get('KDISJ', '1'))
    KSER = int(os.environ.get('KSER', '1'))
    KA2 = int(os.environ.get('KA2', '1'))
    KPOOLM = int(os.environ.get('KPOOLM', '0'))
    KC1 = int(os.environ.get('KC1', '1'))
    KC1B = int(os.environ.get('KC1B', '1'))
    nc = bass.Bass("TRN2", target_bir_lowering=False)

    def din(name, shape, dt=F32):
        return nc.dram_tensor(name, list(shape), dt, kind="ExternalInput").ap()

    def dint(name, shape, dt):
        return nc.dram_tensor(name, list(shape), dt, kind="Internal").ap()

    x = din("x", [S, D])
    wsrc = {
        "wg1": din("wg1", [D, DFF]), "wu1": din("wu1", [D, DFF]), "wd1": din("wd1", [DFF, D]),
        "win": din("win", [D, 7168]), "wpa": din("wpa", [512, D]), "wpf": din("wpf", [512, D]),
        "wo": din("wo", [D, D]),
        "wg2": din("wg2", [D, DFF]), "wu2": din("wu2", [D, DFF]), "wd2": din("wd2", [DFF, D]),
    }
    lnp = din("lnp", [128, 6, D])
    bqk = din("bqk", [128, 24])
    bgt = din("bgt", [128, 16])
    bvu = din("bvu", [128, 2048])
    ebias = din("ebias", [128, 12, 384])
    c1tw = din("c1tw", [128, 64, 256], BF16)
    r2 = din("r2", [64, 128], BF16)
    cds = din("cds", [128, 256], BF16)
    identd = din("identd", [128, 128], BF16)
    out = nc.dram_tensor("out", [NOWN, D], F32, kind="ExternalOutput").ap()

    wb = {k: dint(k + "_b", v.shape, BF16) for k, v in wsrc.items()}
    s_h1 = dint("s_h1", [NOWN, D], F32)
    s_qT = dint("s_qT", [12, 128, NOWN], BF16)
    s_kT = dint("s_kT", [12, 128, NKV], BF16)
    s_v = dint("s_v", [PAD + NKV, 1536], BF16)
    s_u = dint("s_u", [S, 512], BF16)
    s_gT = dint("s_gT", [16, 128, NOWN], BF16)
    s_B = dint("s_B", [128, 64, 2, 512], BF16)
    s_at = dint("s_at", [4, 128, NOWN], BF16)
    s_ft = dint("s_ft", [4, 128, NOWN], BF16)

    es = contextlib.ExitStack()
    with es:
        kx = Kx(nc, es)

        def sb(name, shape, dt, stack=es):
            return stack.enter_context(nc.sbuf_tensor(name, list(shape), dt))

        NB = 6
        banks = [Tl(es.enter_context(nc.psum_tensor("pb%d" % i, [128, 512], F32))) for i in range(NB)]
        ptr_h = [es.enter_context(nc.psum_tensor("ptr%d" % i, [128, 1024], BF16)) for i in range(2)]
        ptrs = [Tl(ptr_h[i][:, 0:512]) for i in range(2)]
        rot = {"b": 0, "t": 0}

        def nbank():
            b = banks[rot["b"] % NB]
            rot["b"] += 1
            return b

        def ntr():
            b = ptrs[rot["t"] % 2]
            rot["t"] += 1
            return b

        T_wb = {k: Tl(v) for k, v in wb.items()}
        T_h1, T_qT, T_kT, T_v, T_u, T_gT, T_B, T_at, T_ft = (Tl(a) for a in (s_h1, s_qT, s_kT, s_v, s_u, s_gT, s_B, s_at, s_ft))

        ident = Tl(sb("ident", [128, 128], BF16))
        lnp_t = Tl(sb("lnp_t", [128, 6, D], F32))
        bqk_t = Tl(sb("bqk_t", [128, 24], F32))
        bgt_t = Tl(sb("bgt_t", [128, 16], F32))
        epsc = Tl(sb("epsc", [128, 3], F32))
        zer = Tl(sb("zer", [128, 1536], BF16))
        kx.dma("sp", "d:c", ident.ap[:], identd, writes=[ident])
        kx.dma("sp", "d:c", lnp_t.ap[:], lnp, writes=[lnp_t])
        kx.dma("sp", "d:c", bqk_t.ap[:], bqk, writes=[bqk_t])
        kx.dma("sp", "d:c", bgt_t.ap[:], bgt, writes=[bgt_t])
        kx.op("dve", lambda e: e.memset(epsc.ap[:, 0:1], EPS / ALPHA ** 2), writes=[epsc])
        kx.op("dve", lambda e: e.memset(zer.ap[:], 0.0), writes=[zer])
        for i in range(PAD // 128):
            kx.dma("pool", "d:st", s_v[i * 128:(i + 1) * 128, :], zer.ap[:], reads=[zer], writes=[T_v], disjoint=True)

        Twb = {}
        for nm_, nb_ in (("wg1", 6), ("wu1", 6), ("wd1", 6), ("win", 14), ("wpa", 2), ("wpf", 2), ("wo", 2),
                         ("wg2", 6), ("wu2", 6), ("wd2", 6)):
            for b_ in range(nb_):
                Twb[(nm_, b_)] = Tl(wb[nm_])

        def twb(name, blk):
            return Twb[(name, blk)]

        def cast_region(name, r0, r1, c0, c1, blk):
            t = Twb[(name, blk)]
            rstep = 256 if (c1 - c0) <= 512 else 128
            ra = r0
            while ra < r1:
                rb = min(r1, ra + rstep)
                ca = c0
                while ca < c1:
                    cb_ = min(c1, ca + 1024)
                    kx.dma("pool", "d:cc", wb[name][ra:rb, ca:cb_], wsrc[name][ra:rb, ca:cb_], writes=[t], disjoint=True, st=t)
                    ca = cb_
                ra = rb

        def cast_ffn(wg, wu, wd):
            for cb in range(6):
                c0 = cb * 512
                w = min(512, DFF - c0)
                cast_region(wg, 0, D, c0, c0 + w, cb)
                cast_region(wu, 0, D, c0, c0 + w, cb)
            for fb in range(6):
                r0 = fb * 512
                cast_region(wd, r0, min(DFF, r0 + 512), 0, D, fb)

        def cast_cols(name, nblk):
            rows = wsrc[name].shape[0]
            for cb in range(nblk):
                cast_region(name, 0, rows, cb * 512, cb * 512 + 512, cb)

        cast_ffn("wg1", "wu1", "wd1")

        def cast_piece(i):
            if i == 0:
                cast_cols("win", 14)
            elif i == 1:
                cast_cols("wpa", 2)
                cast_cols("wpf", 2)
                cast_cols("wo", 2)
            elif 2 <= i <= 7:
                cb = i - 2
                c0 = cb * 512
                w = min(512, DFF - c0)
                cast_region("wg2", 0, D, c0, c0 + w, cb)
                cast_region("wu2", 0, D, c0, c0 + w, cb)
                r0 = cb * 512
                cast_region("wd2", r0, min(DFF, r0 + 512), 0, D, cb)

        rings = {}
        cnt = {"stg": 0, "sg": 0, "ev": 0, "tmp": 0, "t12": 0}
        T_out = Tl(out)

        class Tok:
            pass

        def mk_tok(stack, key, nxf, p1):
            def al(name, shape, dt):
                if kx.dry:
                    return Tl(_Any())
                return Tl(sb(key + name, shape, dt, stack))
            t = Tok()
            t.wslots = [al("wsl%d" % i, [128, 4096], BF16) for i in range(5)]
            if key not in rings:
                rings[key] = WRing(kx, t.wslots)
            rings[key].slots = t.wslots
            t.ring = rings[key]
            t.xf = [[al("xf%d_%d" % (b_, j), [128, D], F32) for j in range(4)] for b_ in range(nxf)]
            t.xb = [al("xb%d" % j, [128, D], BF16) for j in range(4)]
            t.xT = [al("xT%d" % k, [128, CH], BF16) for k in range(8)]
            t.hT = [al("hT%d" % f, [128, CH], BF16) for f in range(NFC)]
            t.sg = [al("sg%d" % i, [128, CH], F32) for i in range(2)]
            t.tmpA = [al("tmpA%d" % i, [128, D], F32) for i in range(2)]
            t.stats = []
            for j in range(4):
                if kx.dry:
                    t.stats.append((Tl(_Any()), Tl(_Any()), _Any()))
                else:
                    st_ = sb(key + "stats%d" % j, [128, 2, 6], F32, stack)
                    t.stats.append((Tl(st_[:, 0, :]), Tl(st_[:, 1, :]), st_))
            t.mv = al("mv", [128, 4, 2], F32)
            t.lnw = al("lnw", [128, 12], F32)
            if p1:
                t.stg = [al("stg%d" % i, [128, 4, CH], BF16) for i in range(3)]
                t.bvu = al("bvu_t", [128, 2048], F32)
                t.h1b = [al("h1b%d" % j, [128, D], BF16) for j in range(4)]
                t.h1T = [al("h1T%d" % k, [128, CH], BF16) for k in range(8)]
            return t

        def evac_copy(dst, dst_ap, src, src_ap):
            cnt["ev"] += 1
            if cnt["ev"] % 2:
                kx.op("dve", lambda e: e.tensor_copy(out=dst_ap, in_=src_ap), reads=[src], writes=[dst])
            else:
                kx.op("act", lambda e: e.activation(out=dst_ap, in_=src_ap, func=AF.Copy), reads=[src], writes=[dst])

        def to_fm(srcb, dstT):
            for dc in range(8):
                p = ntr()
                for j in range(4):
                    kx.op("pe", lambda e: e.transpose(out=p.ap[:, j * 128:(j + 1) * 128],
                                                      in_=srcb[j].ap[:, dc * 128:(dc + 1) * 128], identity=ident.ap[:]),
                          reads=[srcb[j], ident], writes=[p] if j == 0 else [], signal=(j == 3))
                evac_copy(dstT[dc], dstT[dc].ap[:], p, p.ap[:])

        def wblock_cols(name, c0, w, kchunks=8):
            src = wb[name][:, c0:c0 + w].rearrange("(k p) n -> p k n", p=128)
            return [(lambda a: a[:, 0:kchunks * w].rearrange("p (k n) -> p k n", k=kchunks), src, twb(name, c0 // 512))]

        def ffn(tk, X, wg, wu, wd, lni, cres, after_ln):
            ffn_gu(tk, wg, wu)
            ffn_down(tk, X, wd, lni, cres, after_ln)

        def ffn_gu(tk, wg, wu):
            ring, xT, hT, sg = tk.ring, tk.xT, tk.hT, tk.sg
            for cb in range(6):
                c0 = cb * 512
                w = min(512, DFF - c0)
                bg = ring.next(wblock_cols(wg, c0, w))
                bu = ring.next(wblock_cols(wu, c0, w))
                nf = (w + 127) // 128
                for fl in range(nf):
                    f = cb * 4 + fl
                    m = min(128, w - fl * 128)
                    pg, pu = nbank(), nbank()
                    for (pt, blk) in ((pg, bg), (pu, bu)):
                        for kc in range(8):
                            kx.op("pe", lambda e: e.matmul(
                                pt.ap[0:m, :], lhsT=blk.ap[:, 0:8 * w].rearrange("p (k n) -> p k n", k=8)[:, kc, fl * 128:fl * 128 + m],
                                rhs=xT[kc].ap[:], start=(kc == 0), stop=(kc == 7)),
                                reads=[blk, xT[kc]], writes=[pt] if kc == 0 else [], signal=(kc == 7))
                    s_ = sg[cnt["sg"] % 2]
                    cnt["sg"] += 1
                    kx.op("act", lambda e: e.activation(out=s_.ap[0:m, :], in_=pg.ap[0:m, :], func=AF.Silu), reads=[pg], writes=[s_])
                    kx.op("dve", lambda e: e.tensor_tensor(out=hT[f].ap[0:m, :], in0=s_.ap[0:m, :], in1=pu.ap[0:m, :], op=ALU.mult),
                          reads=[s_, pu], writes=[hT[f]])
        def ffn_down(tk, X, wd, lni, cres, after_ln):
            ring, hT = tk.ring, tk.hT
            for ps_ in range(2):
                acc = [[nbank(), nbank()] for _ in range(2)]
                for fb in range(6):
                    f0 = fb * 4
                    if fb < 5:
                        parts = [(lambda a: a[:, 0:4096].rearrange("p (f n) -> p f n", f=4),
                                  wb[wd][f0 * 128:(f0 + 4) * 128, :].rearrange("(f p) n -> p f n", p=128), twb(wd, fb))]
                    else:
                        parts = [(lambda a: a[:, 0:1024], wb[wd][2560:2688, :], twb(wd, fb)),
                                 (lambda a: a[0:64, 1024:2048], wb[wd][2688:2752, :], twb(wd, fb))]
                    blk = ring.next(parts)
                    nfl = 4 if fb < 5 else 2
                    for fl in range(nfl):
                        f = f0 + fl
                        kf = 128 if f < 21 else 64
                        for jj in range(2):
                            j = 2 * ps_ + jj
                            for half in range(2):
                                last = (fl == nfl - 1 and jj == 1 and half == 1)
                                kx.op("pe", lambda e: e.matmul(
                                    acc[jj][half].ap[:, :], lhsT=hT[f].ap[0:kf, j * 128:(j + 1) * 128],
                                    rhs=blk.ap[0:kf, fl * 1024 + half * 512: fl * 1024 + half * 512 + 512],
                                    start=(f == 0), stop=(f == NFC - 1)),
                                    reads=[blk, hT[f]], writes=[acc[jj][half]] if f in (0, NFC - 1) else [], signal=(f == NFC - 1 or last))
                for jj in range(2):
                    j = 2 * ps_ + jj
                    for half in range(2):
                        hs = slice(half * 512, half * 512 + 512)
                        kx.op("dve", lambda e: e.scalar_tensor_tensor(
                            out=X[j].ap[:, hs], in0=acc[jj][half].ap[:, :], scalar=cres, in1=X[j].ap[:, hs],
                            op0=ALU.mult, op1=ALU.add), reads=[acc[jj][half], X[j]], writes=[X[j]])
                    ln_stats(tk, X, j)
            ln_finish(tk, X, lni, after_ln)

        def ln_stats(tk, X, j):
            s0, s1, sfull = tk.stats[j]
            kx.op("dve", lambda e: e.bn_stats(out=s0.ap, in_=X[j].ap[:, 0:512]), reads=[X[j]], writes=[s0])
            kx.op("dve", lambda e: e.bn_stats(out=s1.ap, in_=X[j].ap[:, 512:1024]), reads=[X[j]], writes=[s1])
            kx.op("dve", lambda e: e.bn_aggr(out=tk.mv.ap[:, j, :], in_=sfull[:]), reads=[s0, s1], writes=[tk.mv])

        def ln_finish(tk, X, lni, after_ln):
            mv, lnw = tk.mv, tk.lnw
            kx.op("act", lambda e: e.activation(out=lnw.ap[:, 0:4], in_=mv.ap[:, :, 1], func=AF.Sqrt, bias=epsc.ap[:, 0:1], scale=1.0),
                  reads=[mv, epsc], writes=[lnw])
            kx.op("dve", lambda e: e.reciprocal(out=lnw.ap[:, 4:8], in_=lnw.ap[:, 0:4]), reads=[lnw], writes=[lnw])
            kx.op("dve", lambda e: e.scalar_tensor_tensor(out=lnw.ap[:, 8:12], in0=mv.ap[:, :, 0], scalar=-1.0, in1=lnw.ap[:, 4:8],
                                                          op0=ALU.mult, op1=ALU.mult), reads=[mv, lnw], writes=[lnw])
            for j in range(4):
                t = tk.tmpA[cnt["tmp"] % 2]
                cnt["tmp"] += 1
                kx.op("act", lambda e: e.activation(out=t.ap[:], in_=X[j].ap[:], func=AF.Identity,
                                                    bias=lnw.ap[:, 8 + j:9 + j], scale=lnw.ap[:, 4 + j:5 + j]),
                      reads=[X[j], lnw], writes=[t])
                kx.op("dve", lambda e: e.tensor_tensor(out=t.ap[:], in0=t.ap[:], in1=lnp_t.ap[:, 2 * lni, :], op=ALU.mult),
                      reads=[t, lnp_t], writes=[t])
                kx.op("dve", lambda e: e.tensor_tensor(out=X[j].ap[:], in0=t.ap[:], in1=lnp_t.ap[:, 2 * lni + 1, :], op=ALU.add),
                      reads=[t, lnp_t], writes=[X[j]])
                after_ln(j)

        def phase1(tk):
            xf, xb, xT, ring = tk.xf, tk.xb, tk.xT, tk.ring
            kx.dma("sp", "d:c", tk.bvu.ap[:], bvu, writes=[tk.bvu])

            def load_x(ci, buf):
                for j in range(4):
                    kx.dma("sp", "d:x", xf[buf][j].ap[:], x[ci * CH + j * 128:ci * CH + (j + 1) * 128, :], writes=[xf[buf][j]])

            def nstg():
                t = tk.stg[cnt["stg"] % 3]
                cnt["stg"] += 1
                return t

            def w_in_stage(wci):
                own = wci < 8
                kv = wci < 10
                t0 = wci * CH
                xT = tk.h1T
                for blk_i in range(14):
                    kind = "q" if blk_i < 3 else "k" if blk_i < 6 else "v" if blk_i < 9 else "u" if blk_i == 9 else "g"
                    if kind in ("q", "g") and not own:
                        continue
                    if kind in ("k", "v") and not kv:
                        continue
                    if wci == 9 and blk_i in (3, 4, 6, 7):
                        continue
                    blk = ring.next(wblock_cols("win", blk_i * 512, 512))
                    bview = lambda b_: b_.ap[:, 0:4096].rearrange("p (k n) -> p k n", k=8)
                    sgt = nstg()
                    if kind in ("q", "k", "g"):
                        for hh in range(4):
                            p = nbank()
                            for kc in range(8):
                                kx.op("pe", lambda e: e.matmul(p.ap[:, :], lhsT=bview(blk)[:, kc, hh * 128:(hh + 1) * 128],
                                                               rhs=xT[kc].ap[:], start=(kc == 0), stop=(kc == 7)),
                                      reads=[blk, xT[kc]], writes=[p] if kc == 0 else [], signal=(kc == 7))
                            if kind == "g":
                                col = (blk_i - 10) * 4 + hh
                                kx.op("act", lambda e: e.activation(out=sgt.ap[:, hh, :], in_=p.ap[:, :], func=AF.Sigmoid,
                                                                    bias=bgt_t.ap[:, col:col + 1], scale=1.0),
                                      reads=[p, bgt_t], writes=[sgt])
                            else:
                                col = blk_i * 4 + hh
                                kx.op("act", lambda e: e.activation(out=sgt.ap[:, hh, :], in_=p.ap[:, :], func=AF.Identity,
                                                                    bias=bqk_t.ap[:, col:col + 1], scale=1.0),
                                      reads=[p, bqk_t], writes=[sgt])
                        if kind == "q":
                            dst, Tt = s_qT[blk_i * 4:blk_i * 4 + 4, :, t0:t0 + CH], T_qT
                        elif kind == "k":
                            dst, Tt = s_kT[(blk_i - 3) * 4:(blk_i - 3) * 4 + 4, :, t0:t0 + CH], T_kT
                        else:
                            dst, Tt = s_gT[(blk_i - 10) * 4:(blk_i - 10) * 4 + 4, :, t0:t0 + CH], T_gT
                        kx.dma("pool", "d:st", dst.rearrange("h p t -> p h t"), sgt.ap[:], reads=[sgt], writes=[Tt], disjoint=True)
                    else:
                        boff = (blk_i - 6) * 512
                        for j in range(4):
                            p = nbank()
                            for kc in range(8):
                                kx.op("pe", lambda e: e.matmul(p.ap[:, :], lhsT=xT[kc].ap[:, j * 128:(j + 1) * 128],
                                                               rhs=bview(blk)[:, kc, :], start=(kc == 0), stop=(kc == 7)),
                                      reads=[blk, xT[kc]], writes=[p] if kc == 0 else [], signal=(kc == 7))
                            kx.op("dve", lambda e: e.tensor_tensor(out=sgt.ap[:, j, :], in0=p.ap[:, :], in1=tk.bvu.ap[:, boff:boff + 512], op=ALU.add),
                                  reads=[p, tk.bvu], writes=[sgt])
                        if kind == "v":
                            dst = s_v[PAD + t0:PAD + t0 + CH, boff:boff + 512].rearrange("(j p) c -> p j c", p=128)
                            Tt = T_v
                        else:
                            dst = s_u[t0:t0 + CH, :].rearrange("(j p) c -> p j c", p=128)
                            Tt = T_u
                        kx.dma("pool", "d:st", dst, sgt.ap[:], reads=[sgt], writes=[Tt], disjoint=True)

            load_x(0, 0)
            for ci in range(16):
                buf = ci % 2
                own = ci < 8
                kv = ci < 10
                t0 = ci * CH
                if ci + 1 < 16:
                    load_x(ci + 1, 1 - buf)
                X = xf[buf]
                for j in range(4):
                    kx.op("act", lambda e: e.activation(out=xb[j].ap[:], in_=X[j].ap[:], func=AF.Copy), reads=[X[j]], writes=[xb[j]])
                to_fm(xb, xT)

                def after_ln1(j):
                    if own:
                        kx.dma("pool", "d:st", s_h1[t0 + j * 128:t0 + (j + 1) * 128, :], X[j].ap[:], reads=[X[j]], writes=[T_h1], disjoint=True)
                    kx.op("act", lambda e: e.activation(out=tk.h1b[j].ap[:], in_=X[j].ap[:], func=AF.Copy), reads=[X[j]], writes=[tk.h1b[j]])

                ffn(tk, X, "wg1", "wu1", "wd1", 0, 0.5 / ALPHA, after_ln1)
                if ci == 0:
                    cast_piece(0)
                elif 8 <= ci <= 14:
                    cast_piece(ci - 7)
                if ci > 0:
                    w_in_stage(ci - 1)
                to_fm(tk.h1b, tk.h1T)
            w_in_stage(15)

        def phase3(tk, stack):
            def al(name, shape, dt):
                if kx.dry:
                    return Tl(_Any())
                return Tl(sb("p3" + name, shape, dt, stack))
            ga = al("ga", [128, 8, CH], BF16)
            gf = al("gf", [128, 8, CH], BF16)
            at = al("at", [128, 4, CH], BF16)
            ft = al("ft", [128, 4, CH], BF16)
            mT = [al("mT%d" % k, [128, CH], BF16) for k in range(8)]
            t12 = [al("t12_%d" % i, [128, CH], F32) for i in range(4)]
            xf, xb, xT, ring = tk.xf, tk.xb, tk.xT, tk.ring
            def load_h1(ci_):
                t0_ = ci_ * CH
                for j in range(4):
                    kx.dma("sp", "d:x", xf[(ci_ % 2) * KC1][j].ap[:], s_h1[t0_ + j * 128:t0_ + (j + 1) * 128, :], reads=[T_h1], writes=[xf[(ci_ % 2) * KC1][j]])

            def load_aux(ci_):
                t0_ = ci_ * CH
                kx.dma("sp", "d:l2", at.ap[:], s_at[:, :, t0_:t0_ + CH].rearrange("h p t -> p h t"), reads=[T_at], writes=[at])
                kx.dma("sp", "d:l2", ft.ap[:], s_ft[:, :, t0_:t0_ + CH].rearrange("h p t -> p h t"), reads=[T_ft], writes=[ft])
                kx.dma("sp", "d:l2", ga.ap[:], s_gT[0:8, :, t0_:t0_ + CH].rearrange("h p t -> p h t"), reads=[T_gT], writes=[ga])
                kx.dma("sp", "d:l2", gf.ap[:], s_gT[8:16, :, t0_:t0_ + CH].rearrange("h p t -> p h t"), reads=[T_gT], writes=[gf])

            def stage_A(ci):
                t0 = ci * CH
                X = xf[ci % 2]
                if ci == 0:
                    load_aux(0)
                for cb in range(2):
                    ba = ring.next(wblock_cols("wpa", cb * 512, 512, 4))
                    bf_ = ring.next(wblock_cols("wpf", cb * 512, 512, 4))
                    v4 = lambda b_: b_.ap[:, 0:2048].rearrange("p (k n) -> p k n", k=4)
                    for dl in range(4):
                        dc = cb * 4 + dl
                        pa, pf = nbank(), nbank()
                        for (pt, blk, src) in ((pa, ba, at), (pf, bf_, ft)):
                            for kc in range(4):
                                kx.op("pe", lambda e: e.matmul(pt.ap[:, :], lhsT=v4(blk)[:, kc, dl * 128:(dl + 1) * 128],
                                                               rhs=src.ap[:, kc, :], start=(kc == 0), stop=(kc == 3)),
                                      reads=[blk, src], writes=[pt] if kc == 0 else [], signal=(kc == 3))
                        ta, tb = t12[(cnt["t12"] % 2) * 2], t12[(cnt["t12"] % 2) * 2 + 1]
                        cnt["t12"] += 1
                        kx.op("dve", lambda e: e.tensor_tensor(out=ta.ap[:], in0=pa.ap[:, :], in1=ga.ap[:, dc, :], op=ALU.mult), reads=[pa, ga], writes=[ta])
                        kx.op("dve", lambda e: e.tensor_tensor(out=tb.ap[:], in0=pf.ap[:, :], in1=gf.ap[:, dc, :], op=ALU.mult), reads=[pf, gf], writes=[tb])
                        kx.op("pool", lambda e: e.tensor_tensor(out=mT[dc].ap[:], in0=ta.ap[:], in1=tb.ap[:], op=ALU.add), reads=[ta, tb], writes=[mT[dc]])
                if ci + 1 < 8:
                    load_aux(ci + 1)
                for half in range(2):
                    blk = ring.next(wblock_cols("wo", half * 512, 512))
                    bview = lambda b_: b_.ap[:, 0:4096].rearrange("p (k n) -> p k n", k=8)
                    for j in range(4):
                        p = nbank()
                        for kc in range(8):
                            kx.op("pe", lambda e: e.matmul(p.ap[:, :], lhsT=mT[kc].ap[:, j * 128:(j + 1) * 128], rhs=bview(blk)[:, kc, :],
                                                           start=(kc == 0), stop=(kc == 7)),
                                  reads=[blk, mT[kc]], writes=[p] if kc == 0 else [], signal=(kc == 7))
                        hs = slice(half * 512, half * 512 + 512)
                        kx.op("dve", lambda e: e.scalar_tensor_tensor(out=X[j].ap[:, hs], in0=p.ap[:, :], scalar=1.0 / ALPHA, in1=X[j].ap[:, hs],
                                                                      op0=ALU.mult, op1=ALU.add), reads=[p, X[j]], writes=[X[j]])
                for j in range(4):
                    ln_stats(tk, X, j)

                def after_ln2(j):
                    kx.op("act", lambda e: e.activation(out=xb[j].ap[:], in_=X[j].ap[:], func=AF.Copy), reads=[X[j]], writes=[xb[j]])

                ln_finish(tk, X, 1, after_ln2)

            def stage_down(ci):
                t0 = ci * CH
                X = xf[ci % 2]

                def after_ln3(j):
                    kx.dma("pool", "d:st", out[t0 + j * 128:t0 + (j + 1) * 128, :], X[j].ap[:], reads=[X[j]], writes=[T_out], disjoint=True)

                ffn_down(tk, X, "wd2", 2, 0.5 / ALPHA, after_ln3)
                if ci + 2 < 8:
                    load_h1(ci + 2)

            load_h1(0)
            load_h1(1)
            stage_A(0)
            for ci in range(8):
                to_fm(xb, xT)
                ffn_gu(tk, "wg2", "wu2")
                if ci + 1 < 8:
                    stage_A(ci + 1)
                stage_down(ci)

        def phase2a(st):
            kbuf = [Tl(sb("kbuf%d" % i, [128, PAD + NKV], BF16, st)) for i in range(2)]
            qbuf = [Tl(sb("qbuf%d" % i, [128, NOWN], BF16, st)) for i in range(2)]
            vbuf = [Tl(sb("vbuf%d" % i, [128, 33, 128], BF16, st)) for i in range(2)]
            eb = Tl(sb("eb", [128, 12, 384], F32, st))
            ones = Tl(sb("ones", [128, 128], BF16, st))
            accn = Tl(sb("accn", [128, NOWN], F32, st))
            accd = Tl(sb("accd", [128, NOWN], F32, st))
            exs = [Tl(sb("exs%d" % i, [128, 256], F32, st)) for i in range(3)]
            pTs = [Tl(sb("pTs%d" % i, [128, 256], BF16, st)) for i in range(6)]
            rec = [Tl(sb("rec%d" % i, [128, CH], F32, st)) for i in range(2)]
            ast = [Tl(sb("ast%d" % i, [128, CH], BF16, st)) for i in range(2)]
            kx.dma("sp", "d:c", eb.ap[:], ebias, writes=[eb])
            ebh = Tl(sb("ebh", [128, 12, 384], BF16, st))
            ebl = Tl(sb("ebl", [128, 12, 384], BF16, st))
            ebt = Tl(sb("ebt", [128, 384], F32, st))
            for h_ in range(12):
                kx.op("dve", lambda e: e.tensor_scalar_mul(out=eb.ap[:, h_, :], in0=eb.ap[:, h_, :], scalar1=float(128.0 ** 0.5)), reads=[eb], writes=[eb])
                kx.op("dve", lambda e: e.tensor_copy(out=ebh.ap[:, h_, :], in_=eb.ap[:, h_, :]), reads=[eb], writes=[ebh])
                kx.op("dve", lambda e: e.tensor_copy(out=ebt.ap[:], in_=ebh.ap[:, h_, :]), reads=[ebh], writes=[ebt])
                kx.op("dve", lambda e: e.tensor_tensor(out=ebl.ap[:, h_, :], in0=eb.ap[:, h_, :], in1=ebt.ap[:], op=ALU.subtract), reads=[eb, ebt], writes=[ebl])
            kx.op("dve", lambda e: e.memset(ones.ap[:], 1.0), writes=[ones])
            for i in range(2):
                kx.op("dve", lambda e, i=i: e.memset(kbuf[i].ap[:, 0:PAD], 0.0), writes=[kbuf[i]])
            Sb = [banks[0], banks[1], Tl(ptr_h[0][:, :].bitcast(F32)), Tl(ptr_h[1][:, :].bitcast(F32))]
            NSB = len(Sb)
            Nb = banks[2:4]
            Db = banks[4:6]
            c = {"s": 0, "ex": 0, "pt": 0, "o": 0}
            scale = 128.0 ** -0.5
            heads = [(hp, g) for hp in range(4) for g in range(3)]
            jobs = [(hi, r) for hi, (hp, g) in enumerate(heads) for r in range(DILS[g])]

            def load_head(hi):
                hp, g = heads[hi]
                h = 4 * g + hp
                kb, qb = kbuf[hi % 2], qbuf[hi % 2]
                kx.dma("sp", "d:x", kb.ap[:, PAD:PAD + NKV], s_kT[h, :, :], reads=[T_kT], writes=[kb])
                kx.dma("sp", "d:x", qb.ap[:], s_qT[h, :, :], reads=[T_qT], writes=[qb])

            def load_v(ji):
                hi, r = jobs[ji]
                hp, g = heads[hi]
                h = 4 * g + hp
                d = DILS[g]
                vb = vbuf[ji % 2]
                row0 = PAD + r - 64 * d
                nch = NOWN // d // 128 + 1
                for cc in range(nch):
                    rs = row0 + d * 128 * cc
                    kx.dma("sp", "d:l2", vb.ap[:, cc, :], s_v[ss(rs, 128, d), h * 128:(h + 1) * 128],
                           reads=[T_v], writes=[vb], disjoint=(cc > 0))

            load_head(0)
            load_v(0)
            accst = {"n": True, "d": True}
            for ji, (hi, r) in enumerate(jobs):
                hp, g = heads[hi]
                h = 4 * g + hp
                d = DILS[g]
                kb, qb, vb = kbuf[hi % 2], qbuf[hi % 2], vbuf[ji % 2]
                if r == 0:
                    accst["n"] = True
                    accst["d"] = True
                    if hi + 1 < len(heads):
                        load_head(hi + 1)
                if ji + 1 < len(jobs):
                    load_v(ji + 1)
                nt = NOWN // d // 128
                nch = nt + 1

                def stage1(cc):
                    tl_lo = cc - 1 if cc > 0 else None
                    tl_hi = cc if cc < nt else None
                    q0 = 128 * (cc - 1) if cc > 0 else 0
                    nq = (128 if tl_lo is not None else 0) + (128 if tl_hi is not None else 0)
                    kc0 = PAD + r + d * (-64 + 128 * cc)
                    qc0 = r + d * q0
                    sbk = Sb[c["s"] % NSB]
                    c["s"] += 1
                    if cc == 0:
                        tsl = slice(256, 384)
                    elif cc == nt:
                        tsl = slice(0, 128)
                    else:
                        tsl = slice(0, 256)
                    kx.op("pe", lambda e: e.matmul(sbk.ap[:, 0:nq], lhsT=kb.ap[:, ss(kc0, 128, d)], rhs=qb.ap[:, ss(qc0, nq, d)], start=True, stop=False),
                          reads=[kb, qb], writes=[sbk], signal=False)
                    kx.op("pe", lambda e: e.matmul(sbk.ap[:, 0:nq], lhsT=ident.ap[:], rhs=ebh.ap[:, h, tsl], start=False, stop=False),
                          reads=[ident, ebh], writes=[], signal=False)
                    kx.op("pe", lambda e: e.matmul(sbk.ap[:, 0:nq], lhsT=ident.ap[:], rhs=ebl.ap[:, h, tsl], start=False, stop=True),
                          reads=[ident, ebl], writes=[sbk])
                    pT = pTs[c["pt"] % 6]
                    c["pt"] += 1
                    kx.op("act", lambda e: e.activation(out=pT.ap[:, 0:nq], in_=sbk.ap[:, 0:nq], func=AF.Exp, scale=scale),
                          reads=[sbk], writes=[pT])
                    return (cc, tl_lo, tl_hi, pT)

                def stage2(st_):
                    cc, tl_lo, tl_hi, pT = st_
                    col = 0
                    for (tl, first) in ((tl_lo, False), (tl_hi, True)):
                        if tl is None:
                            continue
                        nbk, dbk = Nb[tl % 2], Db[tl % 2]
                        kx.op("pe", lambda e: e.matmul(nbk.ap[:, 0:128], lhsT=vb.ap[:, cc, :], rhs=pT.ap[:, col:col + 128], start=first, stop=(not first)),
                              reads=[vb, pT], writes=[nbk], signal=(not first))
                        kx.op("pe", lambda e: e.matmul(dbk.ap[:, 0:128], lhsT=ones.ap[:], rhs=pT.ap[:, col:col + 128], start=first, stop=(not first)),
                              reads=[ones, pT], writes=[dbk], signal=(not first))
                        if not first:
                            dsl = ss(r + d * 128 * tl, 128, d)
                            dn, dd_ = (KDISJ and not accst["n"]), (KDISJ and not accst["d"])
                            accst["n"] = False
                            accst["d"] = False
                            if g == 0:
                                kx.op("act", lambda e: e.activation(out=accn.ap[:, dsl], in_=nbk.ap[:, 0:128], func=AF.Copy),
                                      reads=[nbk], writes=[accn], disjoint=dn)
                                kx.op("dve", lambda e: e.tensor_copy(out=accd.ap[:, dsl], in_=dbk.ap[:, 0:128]),
                                      reads=[dbk], writes=[accd], disjoint=dd_)
                            else:
                                kx.op("dve", lambda e: e.tensor_tensor(out=accn.ap[:, dsl], in0=nbk.ap[:, 0:128], in1=accn.ap[:, dsl], op=ALU.add),
                                      reads=[nbk], writes=[accn], disjoint=dn)
                                kx.op("dve", lambda e: e.tensor_tensor(out=accd.ap[:, dsl], in0=dbk.ap[:, 0:128], in1=accd.ap[:, dsl], op=ALU.add),
                                      reads=[dbk], writes=[accd], disjoint=dd_)
                        col += 128

                pend = []
                for cc in range(nch):
                    pend.append(stage1(cc))
                    if len(pend) > KPIPE:
                        stage2(pend.pop(0))
                while pend:
                    stage2(pend.pop(0))
                if not (g == 2 and r == d - 1):
                    continue
                for kb_ in range(8):
                    cs = slice(kb_ * CH, (kb_ + 1) * CH)
                    rc = rec[c["o"] % 2]
                    a_ = ast[c["o"] % 2]
                    c["o"] += 1
                    kx.op("dve", lambda e, rc=rc, cs=cs: e.reciprocal(out=rc.ap[:], in_=accd.ap[:, cs]), reads=[accd], writes=[rc])
                    kx.op("pool", lambda e, rc=rc, cs=cs, a_=a_: e.tensor_tensor(out=a_.ap[:], in0=accn.ap[:, cs], in1=rc.ap[:], op=ALU.mult),
                          reads=[accn, rc], writes=[a_])
                    kx.dma("pool", "d:st", s_at[hp, :, cs], a_.ap[:], reads=[a_], writes=[T_at], disjoint=True)

        def phase2b(st):
            r2t = Tl(sb("r2t", [64, 128], BF16, st))
            cd = Tl(sb("cd", [128, 256], BF16, st))
            for t, s_ in ((r2t, r2), (cd, cds)):
                kx.dma("sp", "d:c", t.ap[:], s_, writes=[t])
            ub = [Tl(sb("ub%d" % i, [128, 8, 512], BF16, st)) for i in range(2)]
            tbk = [Tl(sb("tbk%d" % i, [128, 8, 256], BF16, st)) for i in range(2)]
            stB = [Tl(sb("stB%d" % i, [128, 4, 2, 512], BF16, st)) for i in range(2)]
            Bt = [Tl(sb("Bt%d" % i, [64, 8, 2, 512], BF16, st)) for i in range(2)]
            XT = [Tl(sb("XT%d" % g, [128, 2, NOWN], BF16, st)) for g in range(4)]
            fst = [Tl(sb("fst%d" % i, [128, CH], BF16, st)) for i in range(2)]
            su_v = s_u.rearrange("(a b) c -> a b c", b=64)
            for sb8 in range(8):
                u_ = ub[sb8 % 2]
                tb_ = tbk[sb8 % 2]
                kx.dma("sp", "d:x", u_.ap[:], su_v[:, sb8 * 8:(sb8 + 1) * 8, :], reads=[T_u], writes=[u_])
                kx.dma("sp", "d:x", tb_.ap[:], c1tw[:, sb8 * 8:(sb8 + 1) * 8, :], writes=[tb_])
                for q4 in range(2):
                    sB = stB[(sb8 * 2 + q4) % 2]
                    for sl in range(4):
                        s8 = q4 * 4 + sl
                        pr, pi = nbank(), nbank()
                        kx.op("pe", lambda e: e.matmul(pr.ap[:, :], lhsT=tb_.ap[:, s8, 0:128], rhs=u_.ap[:, s8, :], start=True, stop=True),
                              reads=[tb_, u_], writes=[pr])
                        kx.op("pe", lambda e: e.matmul(pi.ap[:, :], lhsT=tb_.ap[:, s8, 128:256], rhs=u_.ap[:, s8, :], start=True, stop=True),
                              reads=[tb_, u_], writes=[pi])
                        kx.op("dve", lambda e: e.tensor_copy(out=sB.ap[:, sl, 0, :], in_=pr.ap[:, :]), reads=[pr], writes=[sB])
                        kx.op("act", lambda e: e.activation(out=sB.ap[:, sl, 1, :], in_=pi.ap[:, :], func=AF.Copy), reads=[pi], writes=[sB])
                    s20 = sb8 * 8 + q4 * 4
                    kx.dma("pool", "d:st", s_B[:, s20:s20 + 4, :, :], sB.ap[:], reads=[sB], writes=[T_B], disjoint=True)
            for kb in range(16):
                bt = Bt[kb % 2]
                kx.dma("sp", "d:x", bt.ap[:], s_B[kb * 8:(kb + 1) * 8, :, :, :].rearrange("k s r c -> s k r c"), reads=[T_B], writes=[bt])
                for g in range(4):
                    p = nbank()
                    for kl in range(8):
                        kx.op("pe", lambda e, kl=kl, p=p: e.matmul(p.ap[:, kl * 64:(kl + 1) * 64], lhsT=bt.ap[:, kl, 0, g * 128:(g + 1) * 128],
                                                                    rhs=r2t.ap[:, 0:64], start=True, stop=False),
                              reads=[bt, r2t], writes=[p] if kl == 0 else [], signal=False)
                        kx.op("pe", lambda e, kl=kl, p=p: e.matmul(p.ap[:, kl * 64:(kl + 1) * 64], lhsT=bt.ap[:, kl, 1, g * 128:(g + 1) * 128],
                                                                    rhs=r2t.ap[:, 64:128], start=False, stop=True),
                              reads=[bt, r2t], writes=[], signal=(kl == 7))
                    src = p.ap[:, :].rearrange("p (k r j) -> p k r j", k=8, r=2)
                    dst = XT[g].ap[:, :, :].rearrange("p r (j k) -> p k r j", k=128)[:, kb * 8:(kb + 1) * 8, :, :]
                    evac_copy(XT[g], dst, p, src)
            for g in range(4):
                for kb_ in range(8):
                    cs = slice(kb_ * CH, (kb_ + 1) * CH)
                    p = nbank()
                    kx.op("pe", lambda e, p=p, cs=cs: e.matmul(p.ap[:, :], lhsT=cd.ap[:, 0:128], rhs=XT[g].ap[:, 0, cs], start=True, stop=False),
                          reads=[cd, XT[g]], writes=[p], signal=False)
                    kx.op("pe", lambda e, p=p, cs=cs: e.matmul(p.ap[:, :], lhsT=cd.ap[:, 128:256], rhs=XT[g].ap[:, 1, cs], start=False, stop=True),
                          reads=[cd, XT[g]], writes=[])
                    f_ = fst[(g * 8 + kb_) % 2]
                    evac_copy(f_, f_.ap[:], p, p.ap[:, :])
                    kx.dma("pool", "d:st", s_ft[g, :, cs], f_.ap[:], reads=[f_], writes=[T_ft], disjoint=True)

        kx.dry = True
        phase1(mk_tok(None, "a", 2, True))
        phase3(mk_tok(None, "c", 2, False), None)
        kx.dry = False
        rot["b"] = 0
        rot["t"] = 0
        for k_ in cnt:
            cnt[k_] = 0

        import os
        KSTOP = int(os.environ.get("KSTOP", "9"))
        if KSTOP >= 1:
          with contextlib.ExitStack() as s1:
            phase1(mk_tok(s1, "a", 2, True))
            kx.barrier()
        if KSTOP >= 2:
          with contextlib.ExitStack() as s2:
            phase2a(s2)
            kx.barrier()
        if KSTOP >= 3:
          with contextlib.ExitStack() as s2b:
            phase2b(s2b)
            kx.barrier()
        if KSTOP >= 4:
          with contextlib.ExitStack() as s3:
            phase3(mk_tok(s3, "c", 2, False), s3)
            kx.wait_all("pool", [T_out])
            kx.barrier()
    return nc


def _t5_bucket_np(rel):
    half = 16
    ret = (rel > 0).astype(np.int32) * half
    n = np.abs(rel)
    nf = np.maximum(n, 1).astype(np.float32)
    large = 8 + (np.log(nf / np.float32(8)) / np.float32(math.log(128.0)) * np.float32(8)).astype(np.int32)
    large = np.minimum(large, half - 1)
    return ret + np.where(n < 8, n, large)


def _tables(hf):
    bf = ml_dtypes.bfloat16
    fl = (lambda a, n: a) if hf == 0 else (lambda a, n: n - 1 - a)
    i1 = np.arange(128)
    s1 = fl(i1, 128)
    k1 = fl(i1, 128)
    i2 = np.arange(64)
    s2 = fl(i2, 64)
    th = 2 * np.pi * (s1[:, None, None] * k1[None, None, :] / 128.0 + s2[None, :, None] * k1[None, None, :] / 8192.0)
    c1tw = np.concatenate([np.cos(th), np.sin(th)], axis=2).astype(bf)
    j2 = np.arange(32)
    k2 = j2 if hf == 0 else 63 - j2
    th2 = 2 * np.pi * np.outer(s2, k2) / 64.0
    r2 = np.concatenate([np.cos(th2), np.sin(th2), -np.sin(th2), np.cos(th2)], axis=1).astype(bf)
    dd = np.arange(128)
    ph = 2 * np.pi * np.outer(dd, dd) / 128.0
    cds = (np.concatenate([np.cos(ph), -np.sin(ph)], axis=1) / 1024.0).astype(bf)
    return c1tw, r2, cds


def _ebias(rel_bias, hf):
    i = np.arange(128)[:, None]
    j = np.arange(256)[None, :]
    off = i - j + 64
    valid = np.abs(off) <= 64
    sign = 1 if hf == 0 else -1
    eb = np.empty((128, 12, 384), np.float32)
    for h in range(12):
        d = DILS[h // 4]
        b = rel_bias[_t5_bucket_np((off * d * sign).astype(np.int32)), h].astype(np.float32)
        ba = np.where(valid, b, np.float32(-100.0))
        first = ba[:, 128:256].copy()
        first[0:64, :] = -100.0
        eb[:, h, 0:256] = ba
        eb[:, h, 256:384] = first
    return eb


_NC_CACHE = {}


def kernel(x, ln1_g, ln1_b, ffn1_w_gate, ffn1_w_up, ffn1_w_down, w_in, b_in, rel_bias,
           w_proj_attn, w_proj_fourier, w_out, ln2_g, ln2_b, ffn2_w_gate, ffn2_w_up,
           ffn2_w_down, ln3_g, ln3_b):
    f = lambda a: np.ascontiguousarray(np.asarray(a, dtype=np.float32))
    x = f(x)
    b_in0 = f(b_in)[0]
    lnp = np.stack([f(a)[0] for a in (ln1_g, ln1_b, ln2_g, ln2_b, ln3_g, ln3_b)], axis=0)
    lnp = np.ascontiguousarray(np.broadcast_to(lnp[None], (128, 6, D)))
    bqk = np.ascontiguousarray(b_in0[0:3072].reshape(24, 128).T)
    bgt = np.ascontiguousarray(b_in0[5120:7168].reshape(16, 128).T)
    bvu = np.ascontiguousarray(np.broadcast_to(b_in0[3072:5120][None], (128, 2048)))
    rb = f(rel_bias)
    shared = {
        "wg1": f(ffn1_w_gate)[0], "wu1": f(ffn1_w_up)[0], "wd1": f(ffn1_w_down)[0],
        "win": f(w_in)[0], "wpa": f(w_proj_attn)[0], "wpf": f(w_proj_fourier)[0], "wo": f(w_out)[0],
        "wg2": f(ffn2_w_gate)[0], "wu2": f(ffn2_w_up)[0], "wd2": f(ffn2_w_down)[0],
        "lnp": lnp, "bqk": bqk, "bgt": bgt, "bvu": bvu,
        "identd": np.eye(128, dtype=np.float32).astype(ml_dtypes.bfloat16),
    }
    tabs = [_tables(0), _tables(1)]
    ebs = [_ebias(rb, 0), _ebias(rb, 1)]
    in_maps = []
    for c in range(8):
        b, hf = c // 2, c % 2
        xs = x[b] if hf == 0 else x[b, ::-1]
        c1tw, r2, cds = tabs[hf]
        m = dict(shared)
        m.update({"x": np.ascontiguousarray(xs), "ebias": ebs[hf], "c1tw": c1tw, "r2": r2, "cds": cds})
        in_maps.append(m)
    if "nc" not in _NC_CACHE:
        _NC_CACHE["nc"] = build_nc()
    nc = _NC_CACHE["nc"]
    res = run_bass_kernel_spmd(nc, in_maps, core_ids=list(range(8)))
    outp = np.empty((4, S, D), np.float32)
    for c in range(8):
        b, hf = c // 2, c % 2
        o = np.asarray(res.results[c]["out"], dtype=np.float32)
        if hf == 0:
            outp[b, 0:NOWN] = o
        else:
            outp[b, NOWN:] = o[::-1]
    return outp
```
